# Optimizing a Trainium2 kernel written in Bass

```python
import jax, jax.numpy as jnp
from jax import lax
import numpy as np

D_MODEL = 4096
BATCH = 2
SEQ = 8192
DEPTH = 1

MIX_WIDTH = D_MODEL
FOURIER_WIDTH = D_MODEL // 2
FOURIER_GROUPS = 8
FOURIER_GROUP_DIM = FOURIER_WIDTH // FOURIER_GROUPS
MLA_HEADS = 16
MLA_NOPE_DIM = 128
MLA_ROPE_DIM = 64
MLA_V_DIM = 128
MLA_WIDTH = MLA_HEADS * MLA_V_DIM
Q_LORA_RANK = 1024
KV_LORA_RANK = 512
IN_PROJ_WIDTH = FOURIER_WIDTH + Q_LORA_RANK + KV_LORA_RANK + MLA_ROPE_DIM
ROPE_THETA = 10000.0
Q_BLOCK = 128
MEM_TOKENS = 256
XATTN_HEADS = 4
XATTN_HEAD_DIM = D_MODEL // XATTN_HEADS
D_FF = 4 * D_MODEL
NORM_EPS = 1e-6

kernel_name = "hybrid_fnet_mla_memxattn_encoder"


def rms_norm(x, g):
    xf = x.astype(jnp.float32)
    y = xf * lax.rsqrt(jnp.mean(xf * xf, axis=-1, keepdims=True) + NORM_EPS)
    return (y * g.astype(jnp.float32)).astype(x.dtype)


def apply_rope(t, cos, sin):
    half = MLA_ROPE_DIM // 2
    tf = t.astype(jnp.float32)
    t1, t2 = tf[..., :half], tf[..., half:]
    return jnp.concatenate([t1 * cos - t2 * sin, t1 * sin + t2 * cos], axis=-1).astype(t.dtype)


def fourier_mix(z_f, w_fourier):
    b, s, _ = z_f.shape
    zg = z_f.reshape(b, s, FOURIER_GROUPS, FOURIER_GROUP_DIM).astype(jnp.float32)
    zg = jnp.transpose(zg, (0, 2, 1, 3))
    f = jnp.fft.fft2(zg, axes=(-2, -1), norm="ortho").real
    y = jnp.einsum('bgsc,gcd->bsgd', f.astype(z_f.dtype), w_fourier)
    return y.reshape(b, s, FOURIER_WIDTH)


def mla_attention(c_q, c_kv, k_rope, g_q_lora, g_kv_lora, w_uq, w_ukv, cos, sin):
    b, s, _ = c_q.shape
    q = jnp.einsum('bsr,rhd->bshd', rms_norm(c_q, g_q_lora), w_uq)
    q_nope = q[..., :MLA_NOPE_DIM]
    q_rope = apply_rope(q[..., MLA_NOPE_DIM:], cos[:, :, None, :], sin[:, :, None, :])
    kv = jnp.einsum('bsr,rhd->bshd', rms_norm(c_kv, g_kv_lora), w_ukv)
    k_nope = kv[..., :MLA_NOPE_DIM]
    v = kv[..., MLA_NOPE_DIM:]
    k_r = apply_rope(k_rope, cos, sin)
    scale = (MLA_NOPE_DIM + MLA_ROPE_DIM) ** -0.5
    nblk = s // Q_BLOCK
    qn = q_nope.reshape(b, nblk, Q_BLOCK, MLA_HEADS, MLA_NOPE_DIM).swapaxes(0, 1)
    qr = q_rope.reshape(b, nblk, Q_BLOCK, MLA_HEADS, MLA_ROPE_DIM).swapaxes(0, 1)

    def query_block(args):
        qn_b, qr_b = args
        sc = (jnp.einsum('bqhd,bkhd->bhqk', qn_b, k_nope, preferred_element_type=jnp.float32)
              + jnp.einsum('bqhr,bkr->bhqk', qr_b, k_r, preferred_element_type=jnp.float32)) * scale
        p = jax.nn.softmax(sc, axis=-1).astype(v.dtype)
        return jnp.einsum('bhqk,bkhd->bqhd', p, v)

    o = lax.map(query_block, (qn, qr))
    return o.swapaxes(0, 1).reshape(b, s, MLA_WIDTH)


def memory_cross_attention(hn, mem_n, w_xq, w_xk, w_xv, w_xo):
    q = jnp.einsum('bsd,dhe->bshe', hn, w_xq)
    k = jnp.einsum('bmd,dhe->bmhe', mem_n, w_xk)
    v = jnp.einsum('bmd,dhe->bmhe', mem_n, w_xv)
    sc = jnp.einsum('bshe,bmhe->bhsm', q, k, preferred_element_type=jnp.float32) * (XATTN_HEAD_DIM ** -0.5)
    p = jax.nn.softmax(sc, axis=-1).astype(v.dtype)
    o = jnp.einsum('bhsm,bmhe->bshe', p, v)
    return jnp.einsum('bshe,hed->bsd', o, w_xo)


def setup_inputs(seed: int = 0) -> dict:
    key = jax.random.key(seed)
    ks = jax.random.split(key, 24)
    f32 = jnp.float32

    def nrm(k, shape, fan_in):
        return jax.random.normal(k, shape, f32) * (fan_in ** -0.5)

    def gain(k, shape):
        return 1.0 + 0.02 * jax.random.normal(k, shape, f32)

    L = DEPTH
    x = jax.random.normal(ks[0], (BATCH, SEQ, D_MODEL), f32)
    mem = jax.random.normal(ks[1], (BATCH, MEM_TOKENS, D_MODEL), f32)
    offsets = jax.random.randint(ks[2], (BATCH, 1), 0, SEQ, dtype=jnp.int32)
    positions = (jnp.arange(SEQ, dtype=jnp.int32)[None, :] + offsets).astype(jnp.int32)
    return {
        "x": x,
        "mem": mem,
        "positions": positions,
        "g_mix": gain(ks[3], (L, D_MODEL)),
        "w_in": nrm(ks[4], (L, D_MODEL, IN_PROJ_WIDTH), D_MODEL),
        "w_fourier": nrm(ks[5], (L, FOURIER_GROUPS, FOURIER_GROUP_DIM, FOURIER_GROUP_DIM), FOURIER_GROUP_DIM),
        "g_q_lora": gain(ks[6], (L, Q_LORA_RANK)),
        "w_uq": nrm(ks[7], (L, Q_LORA_RANK, MLA_HEADS, MLA_NOPE_DIM + MLA_ROPE_DIM), Q_LORA_RANK),
        "g_kv_lora": gain(ks[8], (L, KV_LORA_RANK)),
        "w_ukv": nrm(ks[9], (L, KV_LORA_RANK, MLA_HEADS, MLA_NOPE_DIM + MLA_V_DIM), KV_LORA_RANK),
        "g_fourier_out": gain(ks[10], (L, FOURIER_WIDTH)),
        "g_mla_out": gain(ks[11], (L, MLA_WIDTH)),
        "w_out": nrm(ks[12], (L, MIX_WIDTH, D_MODEL), MIX_WIDTH),
        "g_xattn": gain(ks[13], (L, D_MODEL)),
        "g_mem": gain(ks[14], (L, D_MODEL)),
        "w_xq": nrm(ks[15], (L, D_MODEL, XATTN_HEADS, XATTN_HEAD_DIM), D_MODEL),
        "w_xk": nrm(ks[16], (L, D_MODEL, XATTN_HEADS, XATTN_HEAD_DIM), D_MODEL),
        "w_xv": nrm(ks[17], (L, D_MODEL, XATTN_HEADS, XATTN_HEAD_DIM), D_MODEL),
        "w_xo": nrm(ks[18], (L, XATTN_HEADS, XATTN_HEAD_DIM, D_MODEL), D_MODEL),
        "g_mlp": gain(ks[19], (L, D_MODEL)),
        "w_ff1": nrm(ks[20], (L, D_MODEL, D_FF), D_MODEL),
        "w_ff2": nrm(ks[21], (L, D_FF, D_MODEL), D_FF),
        "g_final": gain(ks[22], (D_MODEL,)),
    }


def reference(x, mem, positions, g_mix, w_in, w_fourier, g_q_lora, w_uq, g_kv_lora, w_ukv,
              g_fourier_out, g_mla_out, w_out, g_xattn, g_mem, w_xq, w_xk, w_xv, w_xo,
              g_mlp, w_ff1, w_ff2, g_final):
    half = MLA_ROPE_DIM // 2
    inv_freq = ROPE_THETA ** (-jnp.arange(half, dtype=jnp.float32) / half)
    ang = positions.astype(jnp.float32)[..., None] * inv_freq
    cos, sin = jnp.cos(ang), jnp.sin(ang)
    split_pts = [FOURIER_WIDTH, FOURIER_WIDTH + Q_LORA_RANK, FOURIER_WIDTH + Q_LORA_RANK + KV_LORA_RANK]

    h = x
    for layer in range(DEPTH):
        u = rms_norm(h, g_mix[layer])
        z = jnp.einsum('bsd,de->bse', u, w_in[layer])
        z_f, c_q, c_kv, k_rope = jnp.split(z, split_pts, axis=-1)
        y_f = rms_norm(fourier_mix(z_f, w_fourier[layer]), g_fourier_out[layer])
        y_a = rms_norm(mla_attention(c_q, c_kv, k_rope, g_q_lora[layer], g_kv_lora[layer],
                                     w_uq[layer], w_ukv[layer], cos, sin), g_mla_out[layer])
        y = jnp.concatenate([y_f, y_a], axis=-1)
        h = h + jnp.einsum('bse,ed->bsd', y, w_out[layer])
        mem_n = rms_norm(mem, g_mem[layer])
        h = h + memory_cross_attention(rms_norm(h, g_xattn[layer]), mem_n,
                                       w_xq[layer], w_xk[layer], w_xv[layer], w_xo[layer])
        hn = rms_norm(h, g_mlp[layer])
        a = jnp.square(jax.nn.relu(jnp.einsum('bsd,df->bsf', hn, w_ff1[layer])))
        h = h + jnp.einsum('bsf,fd->bsd', a, w_ff2[layer])
    return rms_norm(h, g_final)
```

```python
import math
from contextlib import ExitStack

import numpy as np
import ml_dtypes
import concourse.bass as bass
import concourse.mybir as mybir
from concourse.bass_utils import run_bass_kernel_spmd

F32 = mybir.dt.float32
BF16 = mybir.dt.bfloat16
I32 = mybir.dt.int32
AF = mybir.ActivationFunctionType
ALU = mybir.AluOpType

FULL_CFG = dict(D=4096, S=8192, G=8, H=16, QR=1024, KVR=512, MEM=256, XH=4, DFF=16384)
EPS = 1e-6
TT = 512


class Buf:
    __slots__ = ("name", "w", "wd", "r", "rd")

    def __init__(self, name):
        self.name = name
        self.w = None
        self.wd = []
        self.r = {}
        self.rd = []


class Ev:
    __slots__ = ("eng", "fn", "waits", "inc", "is_dma", "dsem", "dval", "ringwait", "seq")

    def __init__(self, eng, fn, is_dma):
        self.eng = eng
        self.fn = fn
        self.waits = []
        self.inc = False
        self.is_dma = is_dma
        self.dsem = None
        self.dval = 0
        self.ringwait = None
        self.seq = 0


class Sched:
    ENG = ("pe", "act", "dve", "pool", "sp")
    RING = 8

    def __init__(self, nc, stack):
        self.nc = nc
        self.prog = {e: [] for e in self.ENG}
        self.esem = {e: stack.enter_context(nc.semaphore("prog_" + e)) for e in self.ENG}
        self.ring = {q: [stack.enter_context(nc.semaphore(f"dq_{q}_{i}")) for i in range(self.RING)]
                     for q in ("sp", "pool")}
        self.ndma = {"sp": 0, "pool": 0}
        self.out_evs = []
        self.last = {}
        self.recent_dma = {"sp": [], "pool": []}
        self.pending = {}

    def barrier(self):
        deps = []
        for e, ev in self.last.items():
            ev.inc = True
            deps.append(ev)
        for q in ("sp", "pool"):
            deps.extend(self.recent_dma[q])
        for e in self.ENG:
            self.pending[e] = list(deps)

    def _emit(self, eng, fn, reads, writes, is_dma):
        ev = Ev(eng, fn, is_dma)
        deps = []
        for b in reads:
            if b.w is not None:
                deps.append(b.w)
            deps.extend(b.wd)
        for b in writes:
            if b.w is not None:
                deps.append(b.w)
            deps.extend(b.r.values())
            deps.extend(b.rd)
            if not is_dma:
                deps.extend(b.wd)
        bar = self.pending.pop(eng, None)
        seen = set()
        if bar:
            for d in bar:
                if id(d) not in seen:
                    seen.add(id(d))
                    ev.waits.append(d)
        for d in deps:
            if d is ev or id(d) in seen:
                continue
            seen.add(id(d))
            if (not d.is_dma) and (not is_dma) and d.eng == "pe" and eng == "pe":
                continue
            if not d.is_dma:
                d.inc = True
            ev.waits.append(d)
        for b in reads:
            if is_dma:
                b.rd.append(ev)
            else:
                b.r[eng] = ev
        for b in writes:
            if is_dma:
                if b.r or b.rd:
                    b.w = None
                    b.wd = [ev]
                else:
                    b.wd.append(ev)
            else:
                b.w = ev
                b.wd = []
            b.r = {}
            b.rd = []
        if is_dma:
            j = self.ndma[eng]
            self.ndma[eng] = j + 1
            ev.dsem = self.ring[eng][j % self.RING]
            ev.dval = 16 * (j // self.RING + 1)
            if j >= self.RING:
                ev.ringwait = (ev.dsem, 16 * (j // self.RING))
        if is_dma:
            self.recent_dma[eng] = (self.recent_dma[eng] + [ev])[-self.RING:]
        else:
            self.last[eng] = ev
        self.prog[eng].append(ev)
        return ev

    def op(self, eng, method, *args, reads=(), writes=(), **kw):
        return self._emit(eng, lambda e: getattr(e, method)(*args, **kw), reads, writes, False)

    def dma(self, q, out, in_, reads=(), writes=(), is_output=False):
        ev = self._emit(q, lambda e: e.dma_start(out=out, in_=in_), reads, writes, True)
        if is_output:
            self.out_evs.append(ev)
        return ev

    def finish(self, block):
        handles = {"pe": self.nc.tensor, "act": self.nc.scalar, "dve": self.nc.vector,
                   "pool": self.nc.gpsimd, "sp": self.nc.sync}
        for e in self.ENG:
            n = 0
            for ev in self.prog[e]:
                if (not ev.is_dma) and ev.inc:
                    n += 1
                    ev.seq = n
        final_waits = [(ev.dsem, ev.dval) for ev in self.out_evs]

        def run(e, h):
            known = {}

            def wait(sem, val):
                k = id(sem)
                if known.get(k, 0) >= val:
                    return
                known[k] = val
                h.wait_ge(sem, val)

            for ev in self.prog[e]:
                for d in ev.waits:
                    if d.is_dma:
                        wait(d.dsem, d.dval)
                    else:
                        wait(self.esem[d.eng], d.seq)
                if ev.is_dma:
                    if ev.ringwait is not None:
                        wait(*ev.ringwait)
                    ev.fn(h).then_inc(ev.dsem, 16)
                else:
                    r = ev.fn(h)
                    if ev.inc:
                        r.then_inc(self.esem[e], 1)
            if e == "sp":
                for sem, val in final_waits:
                    wait(sem, val)

        @block.tensor
        def _(h):
            run("pe", h)

        @block.scalar
        def _(h):
            run("act", h)

        @block.vector
        def _(h):
            run("dve", h)

        @block.gpsimd
        def _(h):
            run("pool", h)

        @block.sync
        def _(h):
            run("sp", h)


def build_nc(cfg):
    D, S, G, H = cfg["D"], cfg["S"], cfg["G"], cfg["H"]
    QR, KVR, MEM, XH, DFF = cfg["QR"], cfg["KVR"], cfg["MEM"], cfg["XH"], cfg["DFF"]
    FW = G * 256
    AW = H * 128
    assert FW + AW == D and FW == D // 2
    XHD = D // XH
    XC = XHD // 128
    OWN = S // 4
    DC = D // 128
    HC = D // 2
    HCC = HC // 128
    QC = QR // 128
    KVC = KVR // 128
    INW = FW + QR + KVR + 64
    NT_B = S // TT
    NT_O = OWN // TT
    QW = H * 192
    FB = cfg.get("FB", 1024)
    FBC = FB // 128
    att_scale = 192 ** -0.5
    x_scale = XHD ** -0.5
    TWO_PI = 2.0 * math.pi

    nc = bass.Bass("TRN2", target_bir_lowering=False)

    def din(name, shape, dt=F32):
        return nc.dram_tensor(name, list(shape), dt, kind="ExternalInput").ap()

    def dscr(name, shape, dt):
        return nc.dram_tensor(name, list(shape), dt, kind="Internal").ap()

    xb = din("xb", [S, D])
    memb = din("memb", [MEM, D])
    posT = din("posT", [128, S // 128], I32)
    w_in = din("w_in", [D, INW])
    w_f = din("w_f", [G * 256, 256])
    w_uq = din("w_uq", [QR, QW])
    w_ukv = din("w_ukv", [KVR, H * 256])
    w_out = din("w_out", [D, D])
    w_xq = din("w_xq", [D, D])
    w_xk = din("w_xk", [D, D])
    w_xv = din("w_xv", [D, D])
    w_xo = din("w_xo", [D, D])
    w_ff1 = din("w_ff1", [D, DFF])
    w_ff2 = din("w_ff2", [DFF, D])
    gT_mix = din("gT_mix", [128, DC])
    gT_q = din("gT_q", [128, QC])
    gT_kv = din("gT_kv", [128, KVC])
    gT_y = din("gT_y", [128, DC])
    gT_xat = din("gT_xat", [128, DC])
    gT_mem = din("gT_mem", [128, DC])
    gT_mlp = din("gT_mlp", [128, DC])
    g_fin = din("g_fin", [D])
    c_ident = din("c_ident", [128, 128])
    c_invf = din("c_invf", [128, 32])
    c_cs = din("c_cs", [256, 512])
    c_fa = din("c_fa", [S // 64, S // 64], BF16)
    c_gt = din("c_gt", [128, S // 128, 2, 32], BF16)
    out = nc.dram_tensor("out", [OWN, D], F32, kind="ExternalOutput").ap()

    PQ = dscr("PQ", [S, G * 512], BF16)
    CKT = (nc.dram_tensor("CKT", [KVR, S], BF16, kind="ExternalOutput").ap() if cfg.get("dbgY") else dscr("CKT", [KVR, S], BF16))
    KRT = dscr("KRT", [64, S], BF16)
    QNT = (nc.dram_tensor("QNT", [H, 128, OWN], BF16, kind="ExternalOutput").ap() if cfg.get("dbgY") else dscr("QNT", [H, 128, OWN], BF16))
    QRT = dscr("QRT", [H, 64, OWN], BF16)
    CQ = dscr("CQ", [OWN, QR], F32)
    Y = (nc.dram_tensor("Y", [OWN, D], F32, kind="ExternalOutput").ap() if cfg.get("dbgY") else dscr("Y", [OWN, D], F32))
    KXT = dscr("KXT", [D, MEM], BF16)
    w_out_b = dscr("w_out_b", [D, D], BF16)
    w_xq_b = dscr("w_xq_b", [D, D], BF16)
    w_xo_b = dscr("w_xo_b", [D, D], BF16)
    w_ff1_b = dscr("w_ff1_b", [D, DFF], BF16)
    w_ff2_b = dscr("w_ff2_b", [DFF, D], BF16)
    VX = dscr("VX", [MEM, D], BF16)

    with ExitStack() as st:
        sch = Sched(nc, st)
        block = st.enter_context(nc.Block())

        def sb(name, shape, dt):
            return st.enter_context(nc.sbuf_tensor(name, list(shape), dt))

        ps = st.enter_context(nc.psum_tensor("ps", [128, 8, 512], F32))
        banks = [Buf(f"bank{i}") for i in range(8)]
        bank_i = [0]
        bank_set = [list(range(8))]

        def bank():
            lst = bank_set[0]
            i = lst[bank_i[0] % len(lst)]
            bank_i[0] += 1
            return ps[:, i, :], banks[i]

        def bank_at(i):
            return ps[:, i, :], banks[i]

        ident_f = sb("ident_f", [128, 128], F32)
        ident_b = sb("ident_b", [128, 128], BF16)
        ones_b = sb("ones_b", [128, 128], BF16)
        zeros_b = sb("zeros_b", [128, 4], BF16)
        invf = sb("invf", [128, 32], F32)
        posf = sb("posf", [128, S // 128], F32)
        posi = sb("posi", [128, S // 128], I32)
        gmix = sb("gmix", [128, DC], F32)
        gq = sb("gq", [128, QC], F32)
        gkv = sb("gkv", [128, KVC], F32)
        gy = sb("gy", [128, DC], F32)
        gxat = sb("gxat", [128, DC], F32)
        gmem = sb("gmem", [128, DC], F32)
        gmlp = sb("gmlp", [128, DC], F32)
        eps_t = sb("eps_t", [128, 1], F32)
        B_const = Buf("const")
        for dst, src in ((ident_f, c_ident), (invf, c_invf), (posi, posT), (gmix, gT_mix), (gq, gT_q),
                         (gkv, gT_kv), (gy, gT_y), (gxat, gT_xat), (gmem, gT_mem), (gmlp, gT_mlp)):
            sch.dma("sp", dst[:], src, writes=[B_const])
        sch.op("dve", "tensor_copy", out=ident_b[:], in_=ident_f[:], reads=[B_const], writes=[B_const])
        sch.op("dve", "memset", ones_b[:], 1.0, writes=[B_const])
        sch.op("dve", "memset", zeros_b[:], 0.0, writes=[B_const])
        sch.op("dve", "memset", eps_t[:], EPS, writes=[B_const])
        sch.op("dve", "tensor_copy", out=posf[:], in_=posi[:], reads=[B_const], writes=[B_const])

        NSLAB = 3
        slab_t = [sb(f"slab{i}", [128, 16, 512], BF16) for i in range(NSLAB)]
        slab_b = [Buf(f"slab{i}") for i in range(NSLAB)]
        slab_i = [0]

        def load_slab(src, kc, ncols, src_bufs=()):
            i = slab_i[0] % NSLAB
            slab_i[0] += 1
            t, b = slab_t[i], slab_b[i]
            sch.dma("pool", t[:, 0:kc, 0:ncols], src.rearrange("(c p) n -> p c n", p=128), reads=list(src_bufs), writes=[b])
            return t, b

        actT = sb("actT", [128, DC, TT], BF16)
        B_act = Buf("actT")
        rows = [sb(f"rows{i}", [128, HC], F32) for i in range(2)]
        B_rows = [Buf(f"rows{i}") for i in range(2)]
        rows_i = [0]
        xs_all = sb("xs_all", [128, 4, HC], BF16)
        B_xsr = [Buf(f"xs{i}") for i in range(4)]
        xs_i = [0]
        xs = sb("xjunk", [128, HC], BF16)
        B_xs = Buf("xjunk")
        stat = sb("stat", [128, 16], F32)
        B_stat = Buf("stat")
        stn = sb("stn", [128, 4, 8], F32)
        B_stn = [Buf(f"stn{i}") for i in range(4)]
        stn_i = [0]
        cs4 = sb("cs4", [128, 2, 4, 2, 32], F32)
        B_cs4 = Buf("cs4")
        tr = sb("tr", [128, 5, 32], F32)
        B_tr = Buf("tr")

        def rstd_from_ss(ss_ap, n, dst_ap, bst=None):
            bst = B_stat if bst is None else bst
            sch.op("act", "activation", out=dst_ap, in_=ss_ap, func=AF.Sqrt, scale=1.0 / n, bias=eps_t[:, 0:1],
                   reads=[bst, B_const], writes=[bst])
            sch.op("dve", "reciprocal", out=dst_ap, in_=dst_ap, reads=[bst], writes=[bst])

        def transpose_blocks(src_fn, nblk, dst_fn, g_fn, src_bufs, dst_buf, npart=128):
            i0 = 0
            while i0 < nblk:
                n = min(8, nblk - i0)
                bk, bb = bank()
                bkb = bk.bitcast(BF16)
                for j in range(n):
                    sch.op("pe", "transpose", out=bkb[0:npart, j * 128:(j + 1) * 128], in_=src_fn(i0 + j), identity=ident_b[:],
                           reads=list(src_bufs) + [B_const], writes=[bb])
                view = bkb[0:npart, 0:n * 128].rearrange("p (n t) -> p n t", t=128)
                dst = dst_fn(i0, n)
                if g_fn is not None:
                    g = g_fn(i0, n)
                    sch.op("dve", "tensor_tensor", out=dst, in0=view, in1=g.unsqueeze(2).to_broadcast([npart, n, 128]), op=ALU.mult,
                           reads=[bb, B_const], writes=[dst_buf])
                else:
                    sch.op("dve", "tensor_copy", out=dst, in_=view, reads=[bb], writes=[dst_buf])
                i0 += n

        def norm_pre(get_half, seg_norm):
            slot = stn_i[0] % 4
            stn_i[0] += 1
            st_, bst = stn[:, slot, :], B_stn[slot]
            hv = [get_half(0), get_half(1)]
            for half in range(2):
                ap, b = hv[half]
                sch.op("act", "activation", out=xs[:], in_=ap, func=AF.Square, accum_out=st_[:, half:half + 1],
                       reads=[b], writes=[B_xs, bst])
            if seg_norm:
                for half in range(2):
                    rstd_from_ss(st_[:, half:half + 1], HC, st_[:, 2 + half:3 + half], bst)
            else:
                sch.op("dve", "tensor_tensor", out=st_[:, 4:5], in0=st_[:, 0:1], in1=st_[:, 1:2], op=ALU.add, reads=[bst], writes=[bst])
                rstd_from_ss(st_[:, 4:5], D, st_[:, 2:3], bst)
                sch.op("dve", "tensor_copy", out=st_[:, 3:4], in_=st_[:, 2:3], reads=[bst], writes=[bst])
            outs = []
            for half in range(2):
                ap, b = hv[half]
                xi = xs_i[0] % 4
                xs_i[0] += 1
                sch.op("dve", "tensor_scalar", out=xs_all[:, xi, :], in0=ap, scalar1=st_[:, 2 + half:3 + half], scalar2=None, op0=ALU.mult,
                       reads=[b, bst], writes=[B_xsr[xi]])
                outs.append(xi)
            return outs

        def norm_post(outs, gT, dstT, dst_buf, tcols):
            for half in range(2):
                xi = outs[half]
                transpose_blocks(lambda k: xs_all[:, xi, k * 128:(k + 1) * 128], HCC,
                                 lambda i0, n: dstT[:, half * HCC + i0: half * HCC + i0 + n, tcols],
                                 lambda i0, n: gT[:, half * HCC + i0: half * HCC + i0 + n],
                                 [B_xsr[xi]], dst_buf)

        def norm_T_rows(get_half, gT, dstT, dst_buf, tcols, seg_norm, src_bufs=()):
            norm_post(norm_pre(get_half, seg_norm), gT, dstT, dst_buf, tcols)

        def load_rows(row_src_fn, src_bufs=()):
            loaded = []
            for half in range(2):
                i = rows_i[0] % 2
                rows_i[0] += 1
                sch.dma("sp", rows[i][:], row_src_fn(half), reads=list(src_bufs), writes=[B_rows[i]])
                loaded.append((rows[i][:], B_rows[i]))
            return loaded

        def norm_T_from_dram(row_src_fn, gT, dstT, dst_buf, tcols, seg_norm, src_bufs=()):
            loaded = load_rows(row_src_fn, src_bufs)
            norm_T_rows(lambda half: loaded[half], gT, dstT, dst_buf, tcols, seg_norm)

        def mm_fm(slab, sbuf_, kc0, kcn, mcols, act, act_b, tsl, bk, bb, first, last):
            for k in range(kcn):
                sch.op("pe", "matmul", bk, lhsT=slab[:, k, mcols], rhs=act[:, kc0 + k, tsl],
                       start=(first and k == 0), stop=(last and k == kcn - 1), reads=[sbuf_, act_b], writes=[bb])

        def mm_tm(act, act_b, kc0, kcn, tsl, slab, sbuf_, ncols, bk, bb, first, last):
            for k in range(kcn):
                sch.op("pe", "matmul", bk[:, 0:ncols], lhsT=act[:, kc0 + k, tsl], rhs=slab[:, k, 0:ncols],
                       start=(first and k == 0), stop=(last and k == kcn - 1), reads=[sbuf_, act_b], writes=[bb])

        def gemm_tm(w_ap, K, c0, ncols, act, act_b, ntc, epilogue, wbuf=()):
            KC = K // 128
            nsl = (KC + 15) // 16
            bks = [bank() for _ in range(ntc)]
            for s in range(nsl):
                kc0 = s * 16
                kcn = min(16, KC - kc0)
                slab, sbuf_ = load_slab(w_ap[kc0 * 128:(kc0 + kcn) * 128, c0:c0 + ncols], kcn, ncols, wbuf)
                for j in range(ntc):
                    mm_tm(act, act_b, kc0, kcn, slice(j * 128, (j + 1) * 128), slab, sbuf_, ncols,
                          bks[j][0], bks[j][1], s == 0, s == nsl - 1)
            for j in range(ntc):
                epilogue(j, bks[j][0], bks[j][1])

        def gemm_fm(w_ap, K, c0, ncols, act, act_b, nt, epilogue, wbuf=()):
            KC = K // 128
            nsl = (KC + 15) // 16
            nm = ncols // 128
            bks = [bank() for _ in range(nm)]
            for s in range(nsl):
                kc0 = s * 16
                kcn = min(16, KC - kc0)
                slab, sbuf_ = load_slab(w_ap[kc0 * 128:(kc0 + kcn) * 128, c0:c0 + ncols], kcn, ncols, wbuf)
                for m in range(nm):
                    mm_fm(slab, sbuf_, kc0, kcn, slice(m * 128, (m + 1) * 128), act, act_b, slice(0, nt),
                          bks[m][0][:, 0:nt], bks[m][1], s == 0, s == nsl - 1)
            for m in range(nm):
                epilogue(m, bks[m][0], bks[m][1])

        evac_i = [0]

        def evac_copy(dst, src, reads, writes):
            evac_i[0] += 1
            if evac_i[0] % 2:
                sch.op("act", "activation", out=dst, in_=src, func=AF.Copy, reads=reads, writes=writes)
            else:
                sch.op("dve", "tensor_copy", out=dst, in_=src, reads=reads, writes=writes)

        B_PQ = Buf("PQ")
        B_CKT = Buf("CKT")
        B_KRT = Buf("KRT")
        B_QNT = Buf("QNT")
        B_QRT = Buf("QRT")
        B_KXT = Buf("KXT")
        B_VX = Buf("VX")
        B_Y = Buf("Y")
        B_CQ = Buf("CQ")

        with ExitStack() as p1:
            def sb1(name, shape, dt):
                return p1.enter_context(nc.sbuf_tensor(name, list(shape), dt))

            AB = sb1("AB", [128, G, 2, 512], BF16)
            B_AB = Buf("AB")
            with ExitStack() as p0:
                def sb0(name, shape, dt):
                    return p0.enter_context(nc.sbuf_tensor(name, list(shape), dt))

                cs_t = sb0("cs_t", [128, 2, 512], BF16)
                B_cs = Buf("cs")
                wf_t = sb0("wf_t", [128, G, 2, 256], BF16)
                B_wf = Buf("wf")
                sch.dma("pool", cs_t[:], c_cs.rearrange("(c p) n -> p c n", p=128), writes=[B_cs])
                sch.dma("pool", wf_t[:], w_f.rearrange("(g c p) n -> p g c n", p=128, c=2), writes=[B_wf])
                for g in range(G):
                    for cc in range(2):
                        bk, bb = bank()
                        for half in range(2):
                            for jc in range(2):
                                sch.op("pe", "matmul", bk[:, half * 256:(half + 1) * 256],
                                       lhsT=cs_t[:, jc, half * 256 + cc * 128: half * 256 + (cc + 1) * 128],
                                       rhs=wf_t[:, g, jc, :], start=(jc == 0), stop=(jc == 1),
                                       reads=[B_cs, B_wf], writes=[bb])
                        evac_copy(AB[:, g, cc, :], bk, [bb], [B_AB])

                memT = sb0("memT", [128, DC, MEM], BF16)
                B_memT = Buf("memT")
                for j in range(MEM // 128):
                    norm_T_from_dram(lambda half: memb[j * 128:(j + 1) * 128, half * HC:(half + 1) * HC],
                                     gmem, memT, B_memT, slice(j * 128, (j + 1) * 128), False)
                kx_sb = sb0("kx_sb", [128, 4, MEM], BF16)
                B_kx = Buf("kx_sb")
                vx_sb = sb0("vx_sb", [128, 512], BF16)
                B_vx = Buf("vx_sb")
                for c0 in range(0, D, 512):
                    def epi_k(m, bk, bb):
                        evac_copy(kx_sb[:, m, :], bk[:, 0:MEM], [bb], [B_kx])
                        if m == 3:
                            sch.dma("sp", KXT[c0:c0 + 512, :].rearrange("(m p) t -> p m t", p=128), kx_sb[:], reads=[B_kx], writes=[B_KXT])
                    gemm_fm(w_xk, D, c0, 512, memT, B_memT, MEM, epi_k)
                for c0 in range(0, D, 512):
                    def epi_v(j, bk, bb):
                        evac_copy(vx_sb[:], bk, [bb], [B_vx])
                        sch.dma("sp", VX[j * 128:(j + 1) * 128, c0:c0 + 512], vx_sb[:], reads=[B_vx], writes=[B_VX])
                    gemm_tm(w_xv, D, c0, 512, memT, B_memT, MEM // 128, epi_v)

            sch.barrier()
            zfT = sb1("zfT", [128, FW // 128, TT], BF16)
            B_zfT = Buf("zfT")
            pq_sb = [sb1(f"pq_sb{i}", [128, G, 512], BF16) for i in range(2)]
            B_pq = [Buf(f"pq_sb{i}") for i in range(2)]
            ckn = sb1("ckn", [128, KVR], BF16)
            B_ckn = Buf("ckn")
            ckT_sb = sb1("ckT_sb", [128, KVC, TT], BF16)
            B_ckT = Buf("ckT_sb")
            krr = sb1("krr", [128, 64], BF16)
            B_krr = Buf("krr")
            krT_sb = sb1("krT_sb", [64, 1, TT], BF16)
            B_krT = Buf("krT_sb")
            rt = sb1("rt", [128, 4, 32], F32)
            B_rt = Buf("rt")
            pq_i = [0]

            def rope_tables(gchunk, par, j):
                pc = posf[:, gchunk:gchunk + 1]
                C1 = 6.28125
                C2 = float(np.float32(TWO_PI - 6.28125))
                MAGIC = 12582912.0
                a = tr[:, 0, :]
                k = tr[:, 1, :]
                r = tr[:, 2, :]
                rc = tr[:, 3, :]
                m = tr[:, 4, :]
                R = [B_tr, B_const]
                W = [B_tr]
                sch.op("dve", "tensor_scalar", out=a, in0=invf[:], scalar1=pc, scalar2=None, op0=ALU.mult, reads=R, writes=W)
                sch.op("dve", "tensor_scalar", out=k, in0=a, scalar1=1.0 / TWO_PI, scalar2=MAGIC, op0=ALU.mult, op1=ALU.add, reads=R, writes=W)
                sch.op("dve", "tensor_scalar", out=k, in0=k, scalar1=MAGIC, scalar2=None, op0=ALU.subtract, reads=R, writes=W)
                sch.op("dve", "scalar_tensor_tensor", out=r, in0=k, scalar=-C1, in1=a, op0=ALU.mult, op1=ALU.add, reads=R, writes=W)
                sch.op("dve", "scalar_tensor_tensor", out=r, in0=k, scalar=-C2, in1=r, op0=ALU.mult, op1=ALU.add, reads=R, writes=W)
                sch.op("dve", "tensor_scalar", out=rc, in0=r, scalar1=math.pi / 2, scalar2=None, op0=ALU.add, reads=R, writes=W)
                sch.op("dve", "tensor_scalar", out=m, in0=rc, scalar1=math.pi, scalar2=-TWO_PI, op0=ALU.is_gt, op1=ALU.mult, reads=R, writes=W)
                sch.op("dve", "tensor_tensor", out=rc, in0=rc, in1=m, op=ALU.add, reads=R, writes=W)
                LIM = 3.1415925
                for t_ in (r, rc):
                    sch.op("dve", "tensor_scalar", out=t_, in0=t_, scalar1=LIM, scalar2=-LIM, op0=ALU.min, op1=ALU.max, reads=R, writes=W)
                sch.op("act", "activation", out=cs4[:, par, j, 0, :], in_=rc, func=AF.Sin, reads=[B_tr], writes=[B_cs4])
                sch.op("act", "activation", out=cs4[:, par, j, 1, :], in_=r, func=AF.Sin, reads=[B_tr], writes=[B_cs4])

            actT2 = sb1("actT2", [128, DC, TT], BF16)
            uTs = [(actT, B_act), (actT2, Buf("actT2"))]

            def chunk_pre(t0, j, par):
                r0 = t0 + j * 128
                loaded = load_rows(lambda half: xb[r0:r0 + 128, half * HC:(half + 1) * HC])
                outs = norm_pre(lambda half: loaded[half], False)
                rope_tables(r0 // 128, par, j)
                return outs

            def chunk_post(outs, j, ub):
                norm_post(outs, gmix, ub[0], ub[1], slice(j * 128, (j + 1) * 128))

            def load_uT(t0, with_rope):
                for j in range(4):
                    chunk_post(chunk_pre(t0, j, 0), j, uTs[0])

            for j in range(4):
                chunk_post(chunk_pre(0, j, 0), j, uTs[0])
            for tt in range(NT_B):
                t0 = tt * TT
                actT_c, B_act_c = uTs[tt % 2]
                nxt = tt + 1
                pend = {}
                nhooks = FW // 512 + 3
                hook_i = [0]

                def hook():
                    hi = hook_i[0]
                    hook_i[0] += 1
                    if nxt >= NT_B:
                        return
                    evs = [e for e in range(5) if min(e, nhooks - 1) == hi]
                    for e in evs:
                        if e >= 1:
                            chunk_post(pend.pop(e - 1), e - 1, uTs[nxt % 2])
                        if e <= 3:
                            pend[e] = chunk_pre(nxt * TT, e, nxt % 2)
                for c0 in range(0, FW, 512):
                    def epi_z(m, bk, bb):
                        evac_copy(zfT[:, c0 // 128 + m, :], bk, [bb], [B_zfT])
                    gemm_fm(w_in, D, c0, 512, actT_c, B_act_c, TT, epi_z)
                    hook()
                for j in range(4):
                    i = pq_i[0] % 2
                    pq_i[0] += 1
                    for g in range(G):
                        bk, bb = bank()
                        for cc in range(2):
                            sch.op("pe", "matmul", bk, lhsT=zfT[:, 2 * g + cc, j * 128:(j + 1) * 128], rhs=AB[:, g, cc, :],
                                   start=(cc == 0), stop=(cc == 1), reads=[B_zfT, B_AB], writes=[bb])
                        evac_copy(pq_sb[i][:, g, :], bk, [bb], [B_pq[i]])
                    r0 = t0 + j * 128
                    sch.dma("sp", PQ[r0:r0 + 128, :], pq_sb[i][:].rearrange("p g c -> p (g c)"), reads=[B_pq[i]], writes=[B_PQ])
                hook()

                def epi_kv(j, bk, bb):
                    sch.op("act", "activation", out=ckn[:], in_=bk[:, 0:KVR], func=AF.Square, accum_out=stat[:, 8:9],
                           reads=[bb], writes=[B_ckn, B_stat])
                    rstd_from_ss(stat[:, 8:9], KVR, stat[:, 9:10])
                    sch.op("dve", "tensor_scalar", out=ckn[:], in0=bk[:, 0:KVR], scalar1=stat[:, 9:10], scalar2=None, op0=ALU.mult,
                           reads=[bb, B_stat], writes=[B_ckn])
                    transpose_blocks(lambda k: ckn[:, k * 128:(k + 1) * 128], KVC,
                                     lambda i0, n: ckT_sb[:, i0:i0 + n, j * 128:(j + 1) * 128],
                                     lambda i0, n: gkv[:, i0:i0 + n], [B_ckn], B_ckT)
                gemm_tm(w_in, D, FW + QR, KVR, actT_c, B_act_c, 4, epi_kv)
                hook()
                sch.dma("sp", CKT[:, t0:t0 + TT].rearrange("(c p) t -> p c t", p=128), ckT_sb[:], reads=[B_ckT], writes=[B_CKT])

                if tt < NT_O:
                    cqst = pq_sb[1][:].rearrange("p g c -> p (g c)").bitcast(F32)
                    assert G * 256 >= 4 * 512 or QR <= 512
                    for c0 in range(0, QR, 512):
                        w_ = min(512, QR - c0)

                        def epi_cq(j, bk, bb):
                            st_ = cqst[:, (j % (G * 256 // 512)) * 512:(j % (G * 256 // 512)) * 512 + w_]
                            evac_copy(st_, bk[:, 0:w_], [bb], [B_pq[1]])
                            sch.dma("sp", CQ[t0 + j * 128:t0 + (j + 1) * 128, c0:c0 + w_], st_, reads=[B_pq[1]], writes=[B_CQ])
                        gemm_tm(w_in, D, FW + c0, w_, actT_c, B_act_c, 4, epi_cq)

                def epi_kr(j, bk, bb):
                    cos = cs4[:, tt % 2, j, 0, :]
                    sin = cs4[:, tt % 2, j, 1, :]
                    t1 = bk[:, 0:32]
                    t2 = bk[:, 32:64]
                    R = [bb, B_cs4, B_rt]
                    sch.op("dve", "tensor_tensor", out=rt[:, 0, :], in0=t1, in1=cos, op=ALU.mult, reads=R, writes=[B_rt])
                    sch.op("dve", "tensor_tensor", out=rt[:, 1, :], in0=t2, in1=sin, op=ALU.mult, reads=R, writes=[B_rt])
                    sch.op("dve", "tensor_tensor", out=rt[:, 2, :], in0=t1, in1=sin, op=ALU.mult, reads=R, writes=[B_rt])
                    sch.op("dve", "tensor_tensor", out=rt[:, 3, :], in0=t2, in1=cos, op=ALU.mult, reads=R, writes=[B_rt])
                    sch.op("dve", "tensor_tensor", out=krr[:, 0:32], in0=rt[:, 0, :], in1=rt[:, 1, :], op=ALU.subtract, reads=[B_rt], writes=[B_krr])
                    sch.op("dve", "tensor_tensor", out=krr[:, 32:64], in0=rt[:, 2, :], in1=rt[:, 3, :], op=ALU.add, reads=[B_rt], writes=[B_krr])
                    transpose_blocks(lambda k: krr[:, :], 1, lambda i0, n: krT_sb[:, 0:1, j * 128:(j + 1) * 128], None,
                                     [B_krr], B_krT, npart=64)
                gemm_tm(w_in, D, FW + QR + KVR, 64, actT_c, B_act_c, 4, epi_kr)
                hook()
                sch.dma("sp", KRT[:, t0:t0 + TT], krT_sb[:, 0, :], reads=[B_krT], writes=[B_KRT])

        sch.barrier()
        with ExitStack() as p1b:
            def sb1b(name, shape, dt):
                return p1b.enter_context(nc.sbuf_tensor(name, list(shape), dt))

            cq_sb = sb1b("cq_sb", [128, 4, QR], F32)
            B_cq = Buf("cq_sb")
            cqn = sb1b("cqn", [128, QR], BF16)
            B_cqn = Buf("cqn")
            cqT = sb1b("cqT", [128, QC, TT], BF16)
            B_cqT = Buf("cqT")
            q_bf = sb1b("q_bf", [128, 4, H, 192], BF16)
            B_qsb = Buf("q_bf")
            qb_r = sb1b("qb_r", [128, H, 64], BF16)
            B_qb = Buf("qb")
            qt4 = sb1b("qt4", [128, H, 4, 32], F32)
            B_qt4 = Buf("qt4")
            qnT_sb = sb1b("qnT_sb", [128, H, 128], BF16)
            B_qnT = Buf("qnT_sb")
            qrT_sb = sb1b("qrT_sb", [64, H, 128], BF16)
            B_qrT = Buf("qrT_sb")
            q_flat = q_bf[:].rearrange("p j h d -> p j (h d)")
            for tt in range(NT_O):
                t0 = tt * TT
                for j in range(4):
                    r0 = t0 + j * 128
                    sch.dma("sp", cq_sb[:, j, :], CQ[r0:r0 + 128, :], reads=[B_CQ], writes=[B_cq])
                    rope_tables(r0 // 128, 0, j)
                for j in range(4):
                    sch.op("act", "activation", out=cqn[:], in_=cq_sb[:, j, :], func=AF.Square, accum_out=stat[:, 10:11],
                           reads=[B_cq], writes=[B_cqn, B_stat])
                    rstd_from_ss(stat[:, 10:11], QR, stat[:, 12:13])
                    sch.op("dve", "tensor_scalar", out=cqn[:], in0=cq_sb[:, j, :], scalar1=stat[:, 12:13], scalar2=None, op0=ALU.mult,
                           reads=[B_cq, B_stat], writes=[B_cqn])
                    transpose_blocks(lambda k: cqn[:, k * 128:(k + 1) * 128], QC,
                                     lambda i0, n: cqT[:, i0:i0 + n, j * 128:(j + 1) * 128],
                                     lambda i0, n: gq[:, i0:i0 + n], [B_cqn], B_cqT)
                for c0 in range(0, QW, 512):
                    w_ = min(512, QW - c0)

                    def epi_uq(j, bk, bb):
                        evac_copy(q_flat[:, j, c0:c0 + w_], bk[:, 0:w_], [bb], [B_qsb])
                    gemm_tm(w_uq, QR, c0, w_, cqT, B_cqT, 4, epi_uq)
                for j in range(4):
                    cosb = cs4[:, 0, j, 0, :].unsqueeze(1).to_broadcast([128, H, 32])
                    sinb = cs4[:, 0, j, 1, :].unsqueeze(1).to_broadcast([128, H, 32])
                    t1 = q_bf[:, j, :, 128:160]
                    t2 = q_bf[:, j, :, 160:192]
                    R = [B_qsb, B_cs4, B_qt4]
                    sch.op("dve", "tensor_tensor", out=qt4[:, :, 0, :], in0=t1, in1=cosb, op=ALU.mult, reads=R, writes=[B_qt4])
                    sch.op("dve", "tensor_tensor", out=qt4[:, :, 1, :], in0=t2, in1=sinb, op=ALU.mult, reads=R, writes=[B_qt4])
                    sch.op("dve", "tensor_tensor", out=qt4[:, :, 2, :], in0=t1, in1=sinb, op=ALU.mult, reads=R, writes=[B_qt4])
                    sch.op("dve", "tensor_tensor", out=qt4[:, :, 3, :], in0=t2, in1=cosb, op=ALU.mult, reads=R, writes=[B_qt4])
                    sch.op("dve", "tensor_tensor", out=qb_r[:, :, 0:32], in0=qt4[:, :, 0, :], in1=qt4[:, :, 1, :], op=ALU.subtract,
                           reads=[B_qt4], writes=[B_qb])
                    sch.op("dve", "tensor_tensor", out=qb_r[:, :, 32:64], in0=qt4[:, :, 2, :], in1=qt4[:, :, 3, :], op=ALU.add,
                           reads=[B_qt4], writes=[B_qb])
                    transpose_blocks(lambda h: q_bf[:, j, h, 0:128], H, lambda i0, n: qnT_sb[:, i0:i0 + n, :], None, [B_qsb], B_qnT)
                    transpose_blocks(lambda h: qb_r[:, h, :], H, lambda i0, n: qrT_sb[:, i0:i0 + n, :], None, [B_qb], B_qrT, npart=64)
                    c_lo = t0 + j * 128
                    sch.dma("sp", QNT[:, :, c_lo:c_lo + 128].rearrange("h p t -> p h t"), qnT_sb[:], reads=[B_qnT], writes=[B_QNT])
                    sch.dma("sp", QRT[:, :, c_lo:c_lo + 128].rearrange("h p t -> p h t"), qrT_sb[:], reads=[B_qrT], writes=[B_QRT])

        sch.barrier()
        with ExitStack() as p2:
            def sb2(name, shape, dt):
                return p2.enter_context(nc.sbuf_tensor(name, list(shape), dt))

            N1 = S // 128
            K2 = 2 * N1
            assert K2 <= 128
            X2 = sb2("X2", [128, 128, 128], BF16)
            Xb = [actT[:].rearrange("p c t -> p (c t)")[:, 0:128 * 128].rearrange("p (a b) -> p a b", b=128) if DC * TT >= 128 * 128 else None, X2[:]]
            if Xb[0] is None:
                X3 = sb2("X3", [128, 128, 128], BF16)
                Xb[0] = X3[:]
            B_X = [Buf("X0"), Buf("X1")]
            A_re = slab_t[0][:].rearrange("p c n -> p (c n)")[:, 0:N1 * 128].rearrange("p (k c) -> p k c", c=128)
            A_im = slab_t[1][:].rearrange("p c n -> p (c n)")[:, 0:N1 * 128].rearrange("p (k c) -> p k c", c=128)
            B_A = Buf("A")
            fa = sb2("fa", [128, 128], BF16)
            gt = sb2("gt", [128, N1, 2, 32], BF16)
            B_tab = Buf("ffttab")
            sch.dma("sp", fa[0:K2, 0:K2], c_fa, writes=[B_tab])
            sch.dma("sp", gt[:], c_gt, writes=[B_tab])
            yT_sb = sb2("yT_sb", [128, OWN], BF16)
            B_yT = Buf("yT_sb")
            yst = sb2("yst", [128, OWN // 128, 128], F32)
            B_yst = Buf("yst")
            ncb = 512 // K2
            ngrp = FW // 128
            for cg in range(ngrp):
                g, h2 = cg // 2, cg % 2
                cP = g * 512 + h2 * 128
                cQ = cP + 256
                xb_ = Xb[cg % 2]
                bx = B_X[cg % 2]
                sch.dma("sp", xb_[0:N1], PQ[:, cP:cP + 128].rearrange("(a b) c -> a b c", b=128), reads=[B_PQ], writes=[bx])
                sch.dma("sp", xb_[N1:K2], PQ[:, cQ:cQ + 128].rearrange("(a b) c -> a b c", b=128), reads=[B_PQ], writes=[bx])
                for c0 in range(0, 128, ncb):
                    bk, bb = bank()
                    for ci in range(ncb):
                        sch.op("pe", "matmul", bk[:, ci * K2:(ci + 1) * K2], lhsT=xb_[0:K2, :, c0 + ci], rhs=fa[0:K2, 0:K2],
                               start=True, stop=True, reads=[bx, B_tab], writes=[bb])
                    v = bk[:, 0:ncb * K2].rearrange("p (c r k) -> p c r k", r=2, k=N1)
                    sch.op("act", "activation", out=A_re[:, :, c0:c0 + ncb], in_=v[:, :, 0, :].rearrange("p c k -> p k c"), func=AF.Copy,
                           reads=[bb], writes=[B_A])
                    sch.op("dve", "tensor_copy", out=A_im[:, :, c0:c0 + ncb], in_=v[:, :, 1, :].rearrange("p c k -> p k c"),
                           reads=[bb], writes=[B_A])
                yv = yT_sb[:].rearrange("p (b a) -> p a b", a=N1)
                for k0 in range(0, N1, 16):
                    bk, bb = bank()
                    for kk in range(16):
                        k1 = k0 + kk
                        sch.op("pe", "matmul", bk[:, kk * 32:(kk + 1) * 32], lhsT=A_re[:, k1, :], rhs=gt[:, k1, 0, :],
                               start=True, stop=False, reads=[B_A, B_tab], writes=[bb])
                        sch.op("pe", "matmul", bk[:, kk * 32:(kk + 1) * 32], lhsT=A_im[:, k1, :], rhs=gt[:, k1, 1, :],
                               start=False, stop=True, reads=[B_A, B_tab], writes=[bb])
                    evac_copy(yv[:, k0:k0 + 16, :], bk.rearrange("p (a b) -> p a b", b=32), [bb], [B_yT])
                nblk = OWN // 128
                i0 = 0
                while i0 < nblk:
                    n = min(8, nblk - i0)
                    bk, bb = bank()
                    bkb = bk.bitcast(BF16)
                    for j in range(n):
                        sch.op("pe", "transpose", out=bkb[:, j * 128:(j + 1) * 128], in_=yT_sb[:, (i0 + j) * 128:(i0 + j + 1) * 128],
                               identity=ident_b[:], reads=[B_yT, B_const], writes=[bb])
                    evac_copy(yst[:, i0:i0 + n, :], bkb[:, 0:n * 128].rearrange("p (n c) -> p n c", c=128), [bb], [B_yst])
                    i0 += n
                sch.dma("sp", Y[:, cg * 128:(cg + 1) * 128].rearrange("(j p) c -> p j c", p=128), yst[:], reads=[B_yst], writes=[B_Y])

        sch.barrier()
        with ExitStack() as p3:
            def sb3(name, shape, dt):
                return p3.enter_context(nc.sbuf_tensor(name, list(shape), dt))

            ckT = sb3("ckT", [128, KVC, S], BF16)
            B_ckTr = Buf("ckT")
            krT = xs_all[:].rearrange("p a n -> p (a n)")[:, 0:S]
            assert 4 * HC >= S
            B_krTr = Buf("krT")
            sch.dma("sp", ckT[:], CKT.rearrange("(c p) t -> p c t", p=128), reads=[B_CKT], writes=[B_ckTr])
            sch.op("dve", "memset", krT[64:128, :], 0.0, writes=[B_krTr])
            sch.dma("sp", krT[0:64, :], KRT, reads=[B_KRT], writes=[B_krTr])
            casts = []
            B_wb = {}
            for nm, src, dst in (("out", w_out, w_out_b), ("xq", w_xq, w_xq_b), ("xo", w_xo, w_xo_b),
                                 ("ff1", w_ff1, w_ff1_b), ("ff2", w_ff2, w_ff2_b)):
                B_wb[nm] = Buf("wb_" + nm)
                R_, C_ = src.shape
                n_ = min(2048, C_)
                rr = max(1, (4 * 1024 * 1024) // C_)
                for r0 in range(0, R_, rr):
                    r1 = min(R_, r0 + rr)
                    casts.append((dst[r0:r1, :].rearrange("r (a n) -> (r a) n", n=n_),
                                  src[r0:r1, :].rearrange("r (a n) -> (r a) n", n=n_), B_wb[nm]))
            ncast_per = (len(casts) + H - 1) // H
            wkv = [sb3(f"wkv{i}", [128, KVC, 256], BF16) for i in range(2)]
            B_wkv = [Buf(f"wkv{i}") for i in range(2)]
            KT = [slab_t[i][:].rearrange("p c n -> p (c n)")[:, 0:S] for i in range(2)]
            B_KT = [Buf(f"KT{i}") for i in range(2)]
            Vh = [actT[:, i * (DC // 2):(i + 1) * (DC // 2), :].rearrange("p c t -> p (c t)")[:, 0:S].rearrange("p (c d) -> p c d", d=128)
                  for i in range(2)]
            B_Vh = [Buf(f"Vh{i}") for i in range(2)]
            qn_t = [sb3(f"qn_t{i}", [128, TT], BF16) for i in range(2)]
            qr_t = [sb3(f"qr_t{i}", [128, TT], BF16) for i in range(2)]
            B_q = [Buf(f"q_t{i}") for i in range(2)]
            for i in range(2):
                sch.op("dve", "memset", qr_t[i][64:128, :], 0.0, writes=[B_q[i]])
            NP = 4
            pT = [sb3(f"pT{i}", [128, TT], BF16) for i in range(NP)]
            B_pT = [Buf(f"pT{i}") for i in range(NP)]
            ones_f = sb3("ones_f", [128, 4], F32)
            sch.op("dve", "memset", ones_f[:], 1.0, writes=[B_const])
            acc = sb3("acc", [128, TT], F32)
            B_acc = Buf("acc")
            rec = sb3("rec", [128, TT], F32)
            B_rec = Buf("rec")
            on = sb3("on", [128, TT], F32)
            B_on = Buf("on")
            ot = sb3("ot", [128, 4, 128], F32)
            B_ot = Buf("ot")
            qi = 0
            pi_ = 0
            qki = 0
            nck = S // 128
            bank_set[0] = [6, 7]
            for h in range(H):
                hb = (h % 2) if not cfg.get("hb0") else 0
                sch.dma("pool", wkv[hb][:], w_ukv[:, h * 256:(h + 1) * 256].rearrange("(c p) n -> p c n", p=128), writes=[B_wkv[hb]])
                for (cdst, csrc, cbuf) in casts[h * ncast_per:(h + 1) * ncast_per]:
                    sch.dma("pool", cdst, csrc, writes=[cbuf])
                for tk in range(S // 512):
                    bk, bb = bank()
                    for kc in range(KVC):
                        sch.op("pe", "matmul", bk, lhsT=wkv[hb][:, kc, 0:128], rhs=ckT[:, kc, tk * 512:(tk + 1) * 512],
                               start=(kc == 0), stop=(kc == KVC - 1), reads=[B_wkv[hb], B_ckTr], writes=[bb])
                    evac_copy(KT[hb][:, tk * 512:(tk + 1) * 512], bk, [bb], [B_KT[hb]])
                for c4 in range(nck // 4):
                    bk, bb = bank()
                    for ci in range(4):
                        c = c4 * 4 + ci
                        for kc in range(KVC):
                            sch.op("pe", "matmul", bk[:, ci * 128:(ci + 1) * 128], lhsT=ckT[:, kc, c * 128:(c + 1) * 128],
                                   rhs=wkv[hb][:, kc, 128:256], start=(kc == 0), stop=(kc == KVC - 1),
                                   reads=[B_wkv[hb], B_ckTr], writes=[bb])
                    evac_copy(Vh[hb][:, c4 * 4:(c4 + 1) * 4, :], bk.rearrange("p (c d) -> p c d", d=128), [bb], [B_Vh[hb]])
                for tq in range(NT_O):
                    qb = qi % 2
                    qi += 1
                    sch.dma("sp", qn_t[qb][:], QNT[h, :, tq * TT:(tq + 1) * TT], reads=[B_QNT], writes=[B_q[qb]])
                    sch.dma("sp", qr_t[qb][0:64, :], QRT[h, :, tq * TT:(tq + 1) * TT], reads=[B_QRT], writes=[B_q[qb]])
                    o_bk, o_bb = bank_at(0)
                    s_bk, s_bb = bank_at(1)

                    def qk(c):
                        nonlocal qki
                        bk, bb = bank_at(2 + qki % 4)
                        qki += 1
                        sch.op("pe", "matmul", bk, lhsT=KT[hb][:, c * 128:(c + 1) * 128], rhs=qn_t[qb][:], start=True, stop=False,
                               reads=[B_KT[hb], B_q[qb]], writes=[bb])
                        sch.op("pe", "matmul", bk, lhsT=krT[:, c * 128:(c + 1) * 128], rhs=qr_t[qb][:], start=False, stop=True,
                               reads=[B_krTr, B_q[qb]], writes=[bb])
                        return bk, bb
                    sch.op("pe", "matmul", s_bk[:, 0:4], lhsT=ones_b[:], rhs=zeros_b[:, 0:4], start=True, stop=False, skip_group_check=True,
                           reads=[B_const], writes=[s_bb])
                    pend_qk = [qk(cc) for cc in range(min(2, nck))]
                    for c in range(nck):
                        cur = pend_qk.pop(0)
                        if c + 2 < nck:
                            pend_qk.append(qk(c + 2))
                        pb = pi_ % NP
                        pi_ += 1
                        sch.op("act", "activation", out=pT[pb][:], in_=cur[0], func=AF.Exp, scale=att_scale,
                               reads=[cur[1]], writes=[B_pT[pb]])
                        sch.op("pe", "matmul", o_bk, lhsT=Vh[hb][:, c, :], rhs=pT[pb][:], start=(c == 0), stop=(c == nck - 1),
                               reads=[B_Vh[hb], B_pT[pb]], writes=[o_bb])
                        if c % 2 == 1:
                            for sub in range(4):
                                sch.op("pe", "matmul", s_bk[:, sub:sub + 1], lhsT=pT[pb][:, sub * 128:(sub + 1) * 128], rhs=ones_b[:, 0:1],
                                       start=False, stop=False, skip_group_check=True,
                                       reads=[B_const, B_pT[pb]], writes=[s_bb])
                        elif c == 0:
                            sch.op("dve", "tensor_copy", out=acc[:], in_=pT[pb][:], reads=[B_pT[pb]], writes=[B_acc])
                        else:
                            sch.op("dve", "tensor_tensor", out=acc[:], in0=acc[:], in1=pT[pb][:], op=ALU.add,
                                   reads=[B_pT[pb], B_acc], writes=[B_acc])
                    for sub in range(4):
                        sch.op("pe", "matmul", s_bk[:, sub:sub + 1], lhsT=acc[:, sub * 128:(sub + 1) * 128], rhs=ones_f[:, 0:1],
                               start=False, stop=True, skip_group_check=True, reads=[B_const, B_acc], writes=[s_bb])
                    sch.op("dve", "reciprocal", out=rec[:, 0:4], in_=s_bk[:, 0:4], reads=[s_bb], writes=[B_rec])
                    evac_copy(on[:], o_bk, [o_bb], [B_on])
                    tb, tbb = bank()
                    for j in range(4):
                        sch.op("pe", "transpose", out=tb[:, j * 128:(j + 1) * 128], in_=on[:, j * 128:(j + 1) * 128], identity=ident_f[:],
                               reads=[B_on, B_const], writes=[tbb])
                    for j in range(4):
                        sch.op("dve", "tensor_scalar", out=ot[:, j, :], in0=tb[:, j * 128:(j + 1) * 128], scalar1=rec[:, j:j + 1], scalar2=None,
                               op0=ALU.mult, reads=[tbb, B_rec], writes=[B_ot])
                    r0 = tq * TT
                    sch.dma("sp", Y[r0:r0 + TT, FW + h * 128:FW + (h + 1) * 128].rearrange("(j p) d -> p j d", p=128), ot[:],
                            reads=[B_ot], writes=[B_Y])
            bank_set[0] = list(range(8))

        sch.barrier()
        with ExitStack() as p4:
            def sb4(name, shape, dt):
                return p4.enter_context(nc.sbuf_tensor(name, list(shape), dt))

            h_sb = sb4("h_sb", [128, 4, D], F32)
            B_h = Buf("h")
            xp = [sb4("xp0", [128, 512], F32)]
            B_xp = [Buf("xp0")]
            qxT = sb4("qxT", [128, XC, TT], BF16)
            B_qx = Buf("qxT")
            oxT = qxT
            B_ox = B_qx
            assert XC * MEM * 2 <= HC * 4 and (MEM // 128) * XHD * 2 <= HC * 4
            kxh = rows[0][:].bitcast(BF16)[:, 0:XC * MEM].rearrange("p (c m) -> p c m", m=MEM)
            B_kxh = B_rows[0]
            vxh = rows[1][:].bitcast(BF16)[:, 0:(MEM // 128) * XHD].rearrange("p (c e) -> p c e", e=XHD)
            B_vxh = B_rows[1]
            pX = [sb4(f"pX{i}", [128, TT], BF16) for i in range(2)]
            B_pX = [Buf(f"pX{i}") for i in range(2)]
            recx = sb4("recx", [128, TT], F32)
            B_recx = Buf("recx")
            aT0 = sb4("aT0", [128, FBC, TT], BF16)
            if XC >= FBC:
                aT = [aT0, qxT]
                B_aT = [Buf("aT0"), B_qx]
            else:
                aT1 = sb4("aT1", [128, FBC, TT], BF16)
                aT = [aT0, aT1]
                B_aT = [Buf("aT0"), Buf("aT1")]
            rl = [recx]
            B_rl = [B_recx]
            xpi = 0
            rli = 0

            def norm_T_from_h(gT):
                for j in range(4):
                    norm_T_rows(lambda half: (h_sb[:, j, half * HC:(half + 1) * HC], B_h), gT, actT, B_act,
                                slice(j * 128, (j + 1) * 128), False, ())

            for tt in range(NT_O):
                t0 = tt * TT
                for j in range(4):
                    r0 = t0 + j * 128
                    norm_T_from_dram(lambda half: Y[r0:r0 + 128, half * HC:(half + 1) * HC],
                                     gy, actT, B_act, slice(j * 128, (j + 1) * 128), True, src_bufs=[B_Y])
                for c0 in range(0, D, 512):
                    def epi_o(j, bk, bb):
                        nonlocal xpi
                        i = 0
                        xpi += 1
                        r0 = t0 + j * 128
                        sch.dma("sp", xp[i][:], xb[r0:r0 + 128, c0:c0 + 512], writes=[B_xp[i]])
                        sch.op("dve", "tensor_tensor", out=h_sb[:, j, c0:c0 + 512], in0=bk, in1=xp[i][:], op=ALU.add,
                               reads=[bb, B_xp[i]], writes=[B_h])
                    gemm_tm(w_out_b, D, c0, 512, actT, B_act, 4, epi_o, wbuf=[B_wb['out']])
                norm_T_from_h(gxat)
                nm = MEM // 128
                assert nm == 2
                for hd in range(XH):
                    e0 = hd * XHD
                    sch.dma("sp", kxh, KXT[e0:e0 + XHD, :].rearrange("(c p) m -> p c m", p=128), reads=[B_KXT], writes=[B_kxh])
                    sch.dma("sp", vxh, VX[:, e0:e0 + XHD].rearrange("(c p) e -> p c e", p=128), reads=[B_VX], writes=[B_vxh])
                    for c0 in range(0, XHD, 512):
                        w_ = min(512, XHD - c0)

                        def epi_xq(m, bk, bb):
                            evac_copy(qxT[:, c0 // 128 + m, :], bk, [bb], [B_qx])
                        gemm_fm(w_xq_b, D, e0 + c0, w_, actT, B_act, TT, epi_xq, wbuf=[B_wb['xq']])
                    s_bk, s_bb = bank()
                    for mc in range(nm):
                        bk, bb = bank()
                        for ec in range(XC):
                            sch.op("pe", "matmul", bk, lhsT=kxh[:, ec, mc * 128:(mc + 1) * 128], rhs=qxT[:, ec, :],
                                   start=(ec == 0), stop=(ec == XC - 1), reads=[B_kxh, B_qx], writes=[bb])
                        sch.op("act", "activation", out=pX[mc][:], in_=bk, func=AF.Exp, scale=x_scale, reads=[bb], writes=[B_pX[mc]])
                        sch.op("pe", "matmul", s_bk, lhsT=ones_b[:], rhs=pX[mc][:], start=(mc == 0), stop=(mc == nm - 1),
                               reads=[B_const, B_pX[mc]], writes=[s_bb])
                    sch.op("dve", "reciprocal", out=recx[:], in_=s_bk, reads=[s_bb], writes=[B_recx])
                    for ec in range(XC):
                        bk, bb = bank()
                        for mc in range(nm):
                            sch.op("pe", "matmul", bk, lhsT=vxh[:, mc, ec * 128:(ec + 1) * 128], rhs=pX[mc][:],
                                   start=(mc == 0), stop=(mc == nm - 1), reads=[B_vxh, B_pX[0], B_pX[1]], writes=[bb])
                        sch.op("dve", "tensor_tensor", out=oxT[:, ec, :], in0=bk, in1=recx[:], op=ALU.mult,
                               reads=[bb, B_recx], writes=[B_ox])
                    for c0 in range(0, D, 512):
                        def epi_xo(j, bk, bb):
                            sch.op("dve", "tensor_tensor", out=h_sb[:, j, c0:c0 + 512], in0=bk, in1=h_sb[:, j, c0:c0 + 512], op=ALU.add,
                                   reads=[bb, B_h], writes=[B_h])
                        gemm_tm(w_xo_b[e0:e0 + XHD, :], XHD, c0, 512, oxT, B_ox, 4, epi_xo, wbuf=[B_wb['xo']])
                norm_T_from_h(gmlp)
                def ffn1(fb):
                    ab = fb % 2
                    for c0 in range(0, FB, 512):
                        def epi_f1(m, bk, bb):
                            sch.op("act", "activation", out=rl[0][:], in_=bk, func=AF.Relu, reads=[bb], writes=[B_rl[0]])
                            sch.op("act", "activation", out=aT[ab][:, c0 // 128 + m, :], in_=rl[0][:], func=AF.Square,
                                   reads=[B_rl[0]], writes=[B_aT[ab]])
                        gemm_fm(w_ff1_b, D, fb * FB + c0, 512, actT, B_act, TT, epi_f1, wbuf=[B_wb['ff1']])

                def ffn2(fb):
                    ab = fb % 2
                    for c0 in range(0, D, 512):
                        def epi_f2(j, bk, bb):
                            sch.op("dve", "tensor_tensor", out=h_sb[:, j, c0:c0 + 512], in0=bk, in1=h_sb[:, j, c0:c0 + 512], op=ALU.add,
                                   reads=[bb, B_h], writes=[B_h])
                        gemm_tm(w_ff2_b[fb * FB:(fb + 1) * FB, :], FB, c0, 512, aT[ab], B_aT[ab], 4, epi_f2, wbuf=[B_wb['ff2']])

                nfb = DFF // FB
                ffn1(0)
                for fb in range(nfb):
                    if fb + 1 < nfb:
                        ffn1(fb + 1)
                    ffn2(fb)
                for j in range(4):
                    for half in range(2):
                        sch.op("act", "activation", out=xs[:], in_=h_sb[:, j, half * HC:(half + 1) * HC], func=AF.Square,
                               accum_out=stat[:, 5 + half:6 + half], reads=[B_h], writes=[B_xs, B_stat])
                    sch.op("dve", "tensor_tensor", out=stat[:, 4:5], in0=stat[:, 5:6], in1=stat[:, 6:7], op=ALU.add, reads=[B_stat], writes=[B_stat])
                    rstd_from_ss(stat[:, 4:5], D, stat[:, 12 + j:13 + j])
                PW = HC // 4
                for c0 in range(0, D, PW):
                    sch.dma("sp", xp[0][:, 0:PW], g_fin[c0:c0 + PW].partition_broadcast(128), writes=[B_xp[0]])
                    i = rows_i[0] % 2
                    rows_i[0] += 1
                    for j in range(4):
                        sch.op("dve", "scalar_tensor_tensor", out=rows[i][:, j * PW:(j + 1) * PW], in0=h_sb[:, j, c0:c0 + PW],
                               scalar=stat[:, 12 + j:13 + j], in1=xp[0][:, 0:PW], op0=ALU.mult, op1=ALU.mult,
                               reads=[B_h, B_stat, B_xp[0]], writes=[B_rows[i]])
                    sch.dma("sp", out[t0:t0 + TT, c0:c0 + PW].rearrange("(j p) n -> p j n", p=128),
                            rows[i][:, 0:4 * PW].rearrange("p (j n) -> p j n", n=PW), reads=[B_rows[i]], is_output=True)
        sch.finish(block)
    return nc


def _consts(cfg, q):
    S = cfg["S"]
    N1 = S // 128
    ident = np.eye(128, dtype=np.float32)
    half = 32
    invf = (np.float32(10000.0) ** (-np.arange(half, dtype=np.float32) / np.float32(half))).astype(np.float32)
    invf = np.broadcast_to(invf[None, :], (128, 32)).copy()
    j = np.arange(256, dtype=np.int64)
    ang = 2.0 * np.pi * ((j[:, None] * j[None, :]) % 256).astype(np.float64) / 256.0
    sc = 1.0 / math.sqrt(256.0 * S)
    cs = np.concatenate([np.cos(ang) * sc, np.sin(ang) * sc], axis=1).astype(np.float32)
    a1 = np.arange(N1, dtype=np.int64)
    angA = 2.0 * np.pi * ((a1[:, None] * a1[None, :]) % N1).astype(np.float64) / N1
    c, s_ = np.cos(angA), np.sin(angA)
    fa = np.block([[c, -s_], [-s_, -c]]).astype(ml_dtypes.bfloat16)
    s2 = np.arange(128, dtype=np.int64)[:, None, None]
    k1 = np.arange(N1, dtype=np.int64)[None, :, None]
    k2 = (32 * q + np.arange(32, dtype=np.int64))[None, None, :]
    angC = 2.0 * np.pi * ((s2 * (k1 + N1 * k2)) % S).astype(np.float64) / S
    tc, ts = np.cos(angC), np.sin(angC)
    m = (q * np.arange(N1)) % 4
    GR = np.where(m[None, :, None] == 0, tc, np.where(m[None, :, None] == 1, -ts, np.where(m[None, :, None] == 2, -tc, ts)))
    GI = np.where(m[None, :, None] == 0, ts, np.where(m[None, :, None] == 1, tc, np.where(m[None, :, None] == 2, -ts, -tc)))
    gt = np.stack([GR, GI], axis=2).astype(ml_dtypes.bfloat16)
    return ident, invf, cs, fa, np.ascontiguousarray(gt)


def _gT(g):
    g = np.asarray(g, dtype=np.float32).reshape(-1)
    return np.ascontiguousarray(g.reshape(-1, 128).T)


def make_in_maps(cfg, inp):
    S, D = cfg["S"], cfg["D"]
    OWN = S // 4
    f = lambda a: np.ascontiguousarray(np.asarray(a, dtype=np.float32))
    shared = {
        "w_in": f(inp["w_in"][0]),
        "w_f": f(inp["w_fourier"][0]).reshape(-1, 256),
        "w_uq": f(inp["w_uq"][0]).reshape(cfg["QR"], -1),
        "w_ukv": f(inp["w_ukv"][0]).reshape(cfg["KVR"], -1),
        "w_out": f(inp["w_out"][0]),
        "w_xq": f(inp["w_xq"][0]).reshape(D, D),
        "w_xk": f(inp["w_xk"][0]).reshape(D, D),
        "w_xv": f(inp["w_xv"][0]).reshape(D, D),
        "w_xo": f(inp["w_xo"][0]).reshape(D, D),
        "w_ff1": f(inp["w_ff1"][0]),
        "w_ff2": f(inp["w_ff2"][0]),
        "gT_mix": _gT(inp["g_mix"][0]),
        "gT_q": _gT(inp["g_q_lora"][0]),
        "gT_kv": _gT(inp["g_kv_lora"][0]),
        "gT_y": _gT(np.concatenate([np.asarray(inp["g_fourier_out"][0]), np.asarray(inp["g_mla_out"][0])])),
        "gT_xat": _gT(inp["g_xattn"][0]),
        "gT_mem": _gT(inp["g_mem"][0]),
        "gT_mlp": _gT(inp["g_mlp"][0]),
        "g_fin": f(inp["g_final"]),
    }
    x = np.asarray(inp["x"], dtype=np.float32)
    mem = np.asarray(inp["mem"], dtype=np.float32)
    pos = np.asarray(inp["positions"]).astype(np.int32)
    maps = []
    for c in range(8):
        b, q = c // 4, c % 4
        ident, invf, cs, fa, gt = _consts(cfg, q)
        xr = np.ascontiguousarray(np.roll(x[b], -q * OWN, axis=0))
        pr = np.roll(pos[b], -q * OWN)
        m = dict(shared)
        m.update({
            "xb": xr,
            "memb": np.ascontiguousarray(mem[b]),
            "posT": np.ascontiguousarray(pr.reshape(-1, 128).T),
            "c_ident": ident, "c_invf": invf, "c_cs": cs, "c_fa": fa, "c_gt": gt,
        })
        maps.append(m)
    return maps


def run(cfg, inp):
    nc = build_nc(cfg)
    maps = make_in_maps(cfg, inp)
    res = run_bass_kernel_spmd(nc, maps, core_ids=list(range(8)))
    S, D = cfg["S"], cfg["D"]
    OWN = S // 4
    outp = np.zeros((2, S, D), dtype=np.float32)
    for c in range(8):
        b, q = c // 4, c % 4
        outp[b, q * OWN:(q + 1) * OWN] = res.results[c]["out"]
    return outp, res


def kernel(**inputs):
    outp, _ = run(FULL_CFG, inputs)
    return outp
```

```python
import math
from contextlib import ExitStack

import numpy as np
import ml_dtypes
import concourse.bass as bass
import concourse.mybir as mybir
from concourse.bass_utils import run_bass_kernel_spmd

F32 = mybir.dt.float32
BF16 = mybir.dt.bfloat16
I32 = mybir.dt.int32
AF = mybir.ActivationFunctionType
ALU = mybir.AluOpType

FULL_CFG = dict(D=4096, S=8192, G=8, H=16, QR=1024, KVR=512, MEM=256, XH=4, DFF=16384)
EPS = 1e-6
TT = 512


class Buf:
    __slots__ = ("name", "w", "wd", "r", "rd")

    def __init__(self, name):
        self.name = name
        self.w = None
        self.wd = []
        self.r = {}
        self.rd = []


class Ev:
    __slots__ = ("eng", "fn", "waits", "inc", "is_dma", "dsem", "dval", "ringwait", "seq")

    def __init__(self, eng, fn, is_dma):
        self.eng = eng
        self.fn = fn
        self.waits = []
        self.inc = False
        self.is_dma = is_dma
        self.dsem = None
        self.dval = 0
        self.ringwait = None
        self.seq = 0


class Sched:
    ENG = ("pe", "act", "dve", "pool", "sp")
    RING = 8

    def __init__(self, nc, stack):
        self.nc = nc
        self.prog = {e: [] for e in self.ENG}
        self.esem = {e: stack.enter_context(nc.semaphore("prog_" + e)) for e in self.ENG}
        self.ring = {q: [stack.enter_context(nc.semaphore(f"dq_{q}_{i}")) for i in range(self.RING)]
                     for q in ("sp", "pool")}
        self.ndma = {"sp": 0, "pool": 0}
        self.out_evs = []
        self.last = {}
        self.recent_dma = {"sp": [], "pool": []}
        self.pending = {}

    def barrier(self):
        deps = []
        for e, ev in self.last.items():
            ev.inc = True
            deps.append(ev)
        for q in ("sp", "pool"):
            deps.extend(self.recent_dma[q])
        for e in self.ENG:
            self.pending[e] = list(deps)

    def _emit(self, eng, fn, reads, writes, is_dma):
        ev = Ev(eng, fn, is_dma)
        deps = []
        for b in reads:
            if b.w is not None:
                deps.append(b.w)
            deps.extend(b.wd)
        for b in writes:
            if b.w is not None:
                deps.append(b.w)
            deps.extend(b.r.values())
            deps.extend(b.rd)
            if not is_dma:
                deps.extend(b.wd)
        bar = self.pending.pop(eng, None)
        seen = set()
        if bar:
            for d in bar:
                if id(d) not in seen:
                    seen.add(id(d))
                    ev.waits.append(d)
        for d in deps:
            if d is ev or id(d) in seen:
                continue
            seen.add(id(d))
            if (not d.is_dma) and (not is_dma) and d.eng == "pe" and eng == "pe":
                continue
            if not d.is_dma:
                d.inc = True
            ev.waits.append(d)
        for b in reads:
            if is_dma:
                b.rd.append(ev)
            else:
                b.r[eng] = ev
        for b in writes:
            if is_dma:
                if b.r or b.rd:
                    b.w = None
                    b.wd = [ev]
                else:
                    b.wd.append(ev)
            else:
                b.w = ev
                b.wd = []
            b.r = {}
            b.rd = []
        if is_dma:
            j = self.ndma[eng]
            self.ndma[eng] = j + 1
            ev.dsem = self.ring[eng][j % self.RING]
            ev.dval = 16 * (j // self.RING + 1)
            if j >= self.RING:
                ev.ringwait = (ev.dsem, 16 * (j // self.RING))
        if is_dma:
            self.recent_dma[eng] = (self.recent_dma[eng] + [ev])[-self.RING:]
        else:
            self.last[eng] = ev
        self.prog[eng].append(ev)
        return ev

    def op(self, eng, method, *args, reads=(), writes=(), **kw):
        return self._emit(eng, lambda e: getattr(e, method)(*args, **kw), reads, writes, False)

    def dma(self, q, out, in_, reads=(), writes=(), is_output=False):
        ev = self._emit(q, lambda e: e.dma_start(out=out, in_=in_), reads, writes, True)
        if is_output:
            self.out_evs.append(ev)
        return ev

    def finish(self, block):
        handles = {"pe": self.nc.tensor, "act": self.nc.scalar, "dve": self.nc.vector,
                   "pool": self.nc.gpsimd, "sp": self.nc.sync}
        for e in self.ENG:
            n = 0
            for ev in self.prog[e]:
                if (not ev.is_dma) and ev.inc:
                    n += 1
                    ev.seq = n
        final_waits = [(ev.dsem, ev.dval) for ev in self.out_evs]

        def run(e, h):
            known = {}

            def wait(sem, val):
                k = id(sem)
                if known.get(k, 0) >= val:
                    return
                known[k] = val
                h.wait_ge(sem, val)

            for ev in self.prog[e]:
                for d in ev.waits:
                    if d.is_dma:
                        wait(d.dsem, d.dval)
                    else:
                        wait(self.esem[d.eng], d.seq)
                if ev.is_dma:
                    if ev.ringwait is not None:
                        wait(*ev.ringwait)
                    ev.fn(h).then_inc(ev.dsem, 16)
                else:
                    r = ev.fn(h)
                    if ev.inc:
                        r.then_inc(self.esem[e], 1)
            if e == "sp":
                for sem, val in final_waits:
                    wait(sem, val)

        @block.tensor
        def _(h):
            run("pe", h)

        @block.scalar
        def _(h):
            run("act", h)

        @block.vector
        def _(h):
            run("dve", h)

        @block.gpsimd
        def _(h):
            run("pool", h)

        @block.sync
        def _(h):
            run("sp", h)


def build_nc(cfg):
    D, S, G, H = cfg["D"], cfg["S"], cfg["G"], cfg["H"]
    QR, KVR, MEM, XH, DFF = cfg["QR"], cfg["KVR"], cfg["MEM"], cfg["XH"], cfg["DFF"]
    FW = G * 256
    AW = H * 128
    assert FW + AW == D and FW == D // 2
    XHD = D // XH
    XC = XHD // 128
    OWN = S // 4
    DC = D // 128
    HC = D // 2
    HCC = HC // 128
    QC = QR // 128
    KVC = KVR // 128
    INW = FW + QR + KVR + 64
    NT_B = S // TT
    NT_O = OWN // TT
    QW = H * 192
    FB = cfg.get("FB", 1024)
    FBC = FB // 128
    att_scale = 192 ** -0.5
    x_scale = XHD ** -0.5
    TWO_PI = 2.0 * math.pi

    nc = bass.Bass("TRN2", target_bir_lowering=False)

    def din(name, shape, dt=F32):
        return nc.dram_tensor(name, list(shape), dt, kind="ExternalInput").ap()

    def dscr(name, shape, dt):
        return nc.dram_tensor(name, list(shape), dt, kind="Internal").ap()

    xb = din("xb", [S, D])
    memb = din("memb", [MEM, D])
    posT = din("posT", [128, S // 128], I32)
    w_in = din("w_in", [D, INW])
    w_f = din("w_f", [G * 256, 256])
    w_uq = din("w_uq", [QR, QW])
    w_ukv = din("w_ukv", [KVR, H * 256])
    w_out = din("w_out", [D, D])
    w_xq = din("w_xq", [D, D])
    w_xk = din("w_xk", [D, D])
    w_xv = din("w_xv", [D, D])
    w_xo = din("w_xo", [D, D])
    w_ff1 = din("w_ff1", [D, DFF])
    w_ff2 = din("w_ff2", [DFF, D])
    gT_mix = din("gT_mix", [128, DC])
    gT_q = din("gT_q", [128, QC])
    gT_kv = din("gT_kv", [128, KVC])
    gT_y = din("gT_y", [128, DC])
    gT_xat = din("gT_xat", [128, DC])
    gT_mem = din("gT_mem", [128, DC])
    gT_mlp = din("gT_mlp", [128, DC])
    g_fin = din("g_fin", [D])
    c_ident = din("c_ident", [128, 128])
    c_invf = din("c_invf", [128, 32])
    c_cs = din("c_cs", [256, 512])
    c_fa = din("c_fa", [S // 64, S // 64], BF16)
    c_gt = din("c_gt", [128, S // 128, 2, 32], BF16)
    out = nc.dram_tensor("out", [OWN, D], F32, kind="ExternalOutput").ap()

    PQ = dscr("PQ", [S, G * 512], BF16)
    CKT = (nc.dram_tensor("CKT", [KVR, S], BF16, kind="ExternalOutput").ap() if cfg.get("dbgY") else dscr("CKT", [KVR, S], BF16))
    KRT = dscr("KRT", [64, S], BF16)
    QNT = (nc.dram_tensor("QNT", [H, 128, OWN], BF16, kind="ExternalOutput").ap() if cfg.get("dbgY") else dscr("QNT", [H, 128, OWN], BF16))
    QRT = dscr("QRT", [H, 64, OWN], BF16)
    CQ = dscr("CQ", [OWN, QR], F32)
    Y = (nc.dram_tensor("Y", [OWN, D], F32, kind="ExternalOutput").ap() if cfg.get("dbgY") else dscr("Y", [OWN, D], F32))
    KXT = dscr("KXT", [D, MEM], BF16)
    w_out_b = dscr("w_out_b", [D, D], BF16)
    w_xq_b = dscr("w_xq_b", [D, D], BF16)
    w_xo_b = dscr("w_xo_b", [D, D], BF16)
    w_ff1_b = dscr("w_ff1_b", [D, DFF], BF16)
    w_ff2_b = dscr("w_ff2_b", [DFF, D], BF16)
    VX = dscr("VX", [MEM, D], BF16)

    with ExitStack() as st:
        sch = Sched(nc, st)
        block = st.enter_context(nc.Block())

        def sb(name, shape, dt):
            return st.enter_context(nc.sbuf_tensor(name, list(shape), dt))

        ps = st.enter_context(nc.psum_tensor("ps", [128, 8, 512], F32))
        banks = [Buf(f"bank{i}") for i in range(8)]
        bank_i = [0]
        bank_set = [list(range(8))]

        def bank():
            lst = bank_set[0]
            i = lst[bank_i[0] % len(lst)]
            bank_i[0] += 1
            return ps[:, i, :], banks[i]

        def bank_at(i):
            return ps[:, i, :], banks[i]

        ident_f = sb("ident_f", [128, 128], F32)
        ident_b = sb("ident_b", [128, 128], BF16)
        ones_b = sb("ones_b", [128, 128], BF16)
        zeros_b = sb("zeros_b", [128, 4], BF16)
        invf = sb("invf", [128, 32], F32)
        posf = sb("posf", [128, S // 128], F32)
        posi = sb("posi", [128, S // 128], I32)
        gmix = sb("gmix", [128, DC], F32)
        gq = sb("gq", [128, QC], F32)
        gkv = sb("gkv", [128, KVC], F32)
        gy = sb("gy", [128, DC], F32)
        gxat = sb("gxat", [128, DC], F32)
        gmem = sb("gmem", [128, DC], F32)
        gmlp = sb("gmlp", [128, DC], F32)
        eps_t = sb("eps_t", [128, 1], F32)
        B_const = Buf("const")
        for dst, src in ((ident_f, c_ident), (invf, c_invf), (posi, posT), (gmix, gT_mix), (gq, gT_q),
                         (gkv, gT_kv), (gy, gT_y), (gxat, gT_xat), (gmem, gT_mem), (gmlp, gT_mlp)):
            sch.dma("sp", dst[:], src, writes=[B_const])
        sch.op("dve", "tensor_copy", out=ident_b[:], in_=ident_f[:], reads=[B_const], writes=[B_const])
        sch.op("dve", "memset", ones_b[:], 1.0, writes=[B_const])
        sch.op("dve", "memset", zeros_b[:], 0.0, writes=[B_const])
        sch.op("dve", "memset", eps_t[:], EPS, writes=[B_const])
        sch.op("dve", "tensor_copy", out=posf[:], in_=posi[:], reads=[B_const], writes=[B_const])

        NSLAB = 3
        slab_t = [sb(f"slab{i}", [128, 16, 512], BF16) for i in range(NSLAB)]
        slab_b = [Buf(f"slab{i}") for i in range(NSLAB)]
        slab_i = [0]

        def load_slab(src, kc, ncols, src_bufs=()):
            i = slab_i[0] % NSLAB
            slab_i[0] += 1
            t, b = slab_t[i], slab_b[i]
            sch.dma("pool", t[:, 0:kc, 0:ncols], src.rearrange("(c p) n -> p c n", p=128), reads=list(src_bufs), writes=[b])
            return t, b

        actT = sb("actT", [128, DC, TT], BF16)
        B_act = Buf("actT")
        rows = [sb(f"rows{i}", [128, HC], F32) for i in range(2)]
        B_rows = [Buf(f"rows{i}") for i in range(2)]
        rows_i = [0]
        xs_all = sb("xs_all", [128, 4, HC], BF16)
        B_xsr = [Buf(f"xs{i}") for i in range(4)]
        xs_i = [0]
        xs = sb("xjunk", [128, HC], BF16)
        B_xs = Buf("xjunk")
        stat = sb("stat", [128, 16], F32)
        B_stat = Buf("stat")
        stn = sb("stn", [128, 4, 8], F32)
        B_stn = [Buf(f"stn{i}") for i in range(4)]
        stn_i = [0]
        cs4 = sb("cs4", [128, 2, 4, 2, 32], F32)
        B_cs4 = Buf("cs4")
        tr = sb("tr", [128, 5, 32], F32)
        B_tr = Buf("tr")

        def rstd_from_ss(ss_ap, n, dst_ap, bst=None):
            bst = B_stat if bst is None else bst
            sch.op("act", "activation", out=dst_ap, in_=ss_ap, func=AF.Sqrt, scale=1.0 / n, bias=eps_t[:, 0:1],
                   reads=[bst, B_const], writes=[bst])
            sch.op("dve", "reciprocal", out=dst_ap, in_=dst_ap, reads=[bst], writes=[bst])

        def transpose_blocks(src_fn, nblk, dst_fn, g_fn, src_bufs, dst_buf, npart=128):
            i0 = 0
            while i0 < nblk:
                n = min(8, nblk - i0)
                bk, bb = bank()
                bkb = bk.bitcast(BF16)
                for j in range(n):
                    sch.op("pe", "transpose", out=bkb[0:npart, j * 128:(j + 1) * 128], in_=src_fn(i0 + j), identity=ident_b[:],
                           reads=list(src_bufs) + [B_const], writes=[bb])
                view = bkb[0:npart, 0:n * 128].rearrange("p (n t) -> p n t", t=128)
                dst = dst_fn(i0, n)
                if g_fn is not None:
                    g = g_fn(i0, n)
                    sch.op("dve", "tensor_tensor", out=dst, in0=view, in1=g.unsqueeze(2).to_broadcast([npart, n, 128]), op=ALU.mult,
                           reads=[bb, B_const], writes=[dst_buf])
                else:
                    sch.op("dve", "tensor_copy", out=dst, in_=view, reads=[bb], writes=[dst_buf])
                i0 += n

        def norm_pre(get_half, seg_norm):
            slot = stn_i[0] % 4
            stn_i[0] += 1
            st_, bst = stn[:, slot, :], B_stn[slot]
            hv = [get_half(0), get_half(1)]
            for half in range(2):
                ap, b = hv[half]
                sch.op("act", "activation", out=xs[:], in_=ap, func=AF.Square, accum_out=st_[:, half:half + 1],
                       reads=[b], writes=[B_xs, bst])
            if seg_norm:
                for half in range(2):
                    rstd_from_ss(st_[:, half:half + 1], HC, st_[:, 2 + half:3 + half], bst)
            else:
                sch.op("dve", "tensor_tensor", out=st_[:, 4:5], in0=st_[:, 0:1], in1=st_[:, 1:2], op=ALU.add, reads=[bst], writes=[bst])
                rstd_from_ss(st_[:, 4:5], D, st_[:, 2:3], bst)
                sch.op("dve", "tensor_copy", out=st_[:, 3:4], in_=st_[:, 2:3], reads=[bst], writes=[bst])
            outs = []
            for half in range(2):
                ap, b = hv[half]
                xi = xs_i[0] % 4
                xs_i[0] += 1
                sch.op("dve", "tensor_scalar", out=xs_all[:, xi, :], in0=ap, scalar1=st_[:, 2 + half:3 + half], scalar2=None, op0=ALU.mult,
                       reads=[b, bst], writes=[B_xsr[xi]])
                outs.append(xi)
            return outs

        def norm_post(outs, gT, dstT, dst_buf, tcols):
            for half in range(2):
                xi = outs[half]
                transpose_blocks(lambda k: xs_all[:, xi, k * 128:(k + 1) * 128], HCC,
                                 lambda i0, n: dstT[:, half * HCC + i0: half * HCC + i0 + n, tcols],
                                 lambda i0, n: gT[:, half * HCC + i0: half * HCC + i0 + n],
                                 [B_xsr[xi]], dst_buf)

        def norm_T_rows(get_half, gT, dstT, dst_buf, tcols, seg_norm, src_bufs=()):
            norm_post(norm_pre(get_half, seg_norm), gT, dstT, dst_buf, tcols)

        def load_rows(row_src_fn, src_bufs=()):
            loaded = []
            for half in range(2):
                i = rows_i[0] % 2
                rows_i[0] += 1
                sch.dma("sp", rows[i][:], row_src_fn(half), reads=list(src_bufs), writes=[B_rows[i]])
                loaded.append((rows[i][:], B_rows[i]))
            return loaded

        def norm_T_from_dram(row_src_fn, gT, dstT, dst_buf, tcols, seg_norm, src_bufs=()):
            loaded = load_rows(row_src_fn, src_bufs)
            norm_T_rows(lambda half: loaded[half], gT, dstT, dst_buf, tcols, seg_norm)

        def mm_fm(slab, sbuf_, kc0, kcn, mcols, act, act_b, tsl, bk, bb, first, last):
            for k in range(kcn):
                sch.op("pe", "matmul", bk, lhsT=slab[:, k, mcols], rhs=act[:, kc0 + k, tsl],
                       start=(first and k == 0), stop=(last and k == kcn - 1), reads=[sbuf_, act_b], writes=[bb])

        def mm_tm(act, act_b, kc0, kcn, tsl, slab, sbuf_, ncols, bk, bb, first, last):
            for k in range(kcn):
                sch.op("pe", "matmul", bk[:, 0:ncols], lhsT=act[:, kc0 + k, tsl], rhs=slab[:, k, 0:ncols],
                       start=(first and k == 0), stop=(last and k == kcn - 1), reads=[sbuf_, act_b], writes=[bb])

        def gemm_tm(w_ap, K, c0, ncols, act, act_b, ntc, epilogue, wbuf=()):
            KC = K // 128
            nsl = (KC + 15) // 16
            bks = [bank() for _ in range(ntc)]
            for s in range(nsl):
                kc0 = s * 16
                kcn = min(16, KC - kc0)
                slab, sbuf_ = load_slab(w_ap[kc0 * 128:(kc0 + kcn) * 128, c0:c0 + ncols], kcn, ncols, wbuf)
                for j in range(ntc):
                    mm_tm(act, act_b, kc0, kcn, slice(j * 128, (j + 1) * 128), slab, sbuf_, ncols,
                          bks[j][0], bks[j][1], s == 0, s == nsl - 1)
            for j in range(ntc):
                epilogue(j, bks[j][0], bks[j][1])

        def gemm_fm(w_ap, K, c0, ncols, act, act_b, nt, epilogue, wbuf=()):
            KC = K // 128
            nsl = (KC + 15) // 16
            nm = ncols // 128
            bks = [bank() for _ in range(nm)]
            for s in range(nsl):
                kc0 = s * 16
                kcn = min(16, KC - kc0)
                slab, sbuf_ = load_slab(w_ap[kc0 * 128:(kc0 + kcn) * 128, c0:c0 + ncols], kcn, ncols, wbuf)
                for m in range(nm):
                    mm_fm(slab, sbuf_, kc0, kcn, slice(m * 128, (m + 1) * 128), act, act_b, slice(0, nt),
                          bks[m][0][:, 0:nt], bks[m][1], s == 0, s == nsl - 1)
            for m in range(nm):
                epilogue(m, bks[m][0], bks[m][1])

        evac_i = [0]

        def evac_copy(dst, src, reads, writes):
            evac_i[0] += 1
            if evac_i[0] % 2:
                sch.op("act", "activation", out=dst, in_=src, func=AF.Copy, reads=reads, writes=writes)
            else:
                sch.op("dve", "tensor_copy", out=dst, in_=src, reads=reads, writes=writes)

        B_PQ = Buf("PQ")
        B_CKT = Buf("CKT")
        B_KRT = Buf("KRT")
        B_QNT = Buf("QNT")
        B_QRT = Buf("QRT")
        B_KXT = Buf("KXT")
        B_VX = Buf("VX")
        B_Y = Buf("Y")
        B_CQ = Buf("CQ")

        with ExitStack() as p1:
            def sb1(name, shape, dt):
                return p1.enter_context(nc.sbuf_tensor(name, list(shape), dt))

            AB = sb1("AB", [128, G, 2, 512], BF16)
            B_AB = Buf("AB")
            with ExitStack() as p0:
                def sb0(name, shape, dt):
                    return p0.enter_context(nc.sbuf_tensor(name, list(shape), dt))

                cs_t = sb0("cs_t", [128, 2, 512], BF16)
                B_cs = Buf("cs")
                wf_t = sb0("wf_t", [128, G, 2, 256], BF16)
                B_wf = Buf("wf")
                sch.dma("pool", cs_t[:], c_cs.rearrange("(c p) n -> p c n", p=128), writes=[B_cs])
                sch.dma("pool", wf_t[:], w_f.rearrange("(g c p) n -> p g c n", p=128, c=2), writes=[B_wf])
                for g in range(G):
                    for cc in range(2):
                        bk, bb = bank()
                        for half in range(2):
                            for jc in range(2):
                                sch.op("pe", "matmul", bk[:, half * 256:(half + 1) * 256],
                                       lhsT=cs_t[:, jc, half * 256 + cc * 128: half * 256 + (cc + 1) * 128],
                                       rhs=wf_t[:, g, jc, :], start=(jc == 0), stop=(jc == 1),
                                       reads=[B_cs, B_wf], writes=[bb])
                        evac_copy(AB[:, g, cc, :], bk, [bb], [B_AB])

                memT = sb0("memT", [128, DC, MEM], BF16)
                B_memT = Buf("memT")
                for j in range(MEM // 128):
                    norm_T_from_dram(lambda half: memb[j * 128:(j + 1) * 128, half * HC:(half + 1) * HC],
                                     gmem, memT, B_memT, slice(j * 128, (j + 1) * 128), False)
                kx_sb = sb0("kx_sb", [128, 4, MEM], BF16)
                B_kx = Buf("kx_sb")
                vx_sb = sb0("vx_sb", [128, 512], BF16)
                B_vx = Buf("vx_sb")
                for c0 in range(0, D, 512):
                    def epi_k(m, bk, bb):
                        evac_copy(kx_sb[:, m, :], bk[:, 0:MEM], [bb], [B_kx])
                        if m == 3:
                            sch.dma("sp", KXT[c0:c0 + 512, :].rearrange("(m p) t -> p m t", p=128), kx_sb[:], reads=[B_kx], writes=[B_KXT])
                    gemm_fm(w_xk, D, c0, 512, memT, B_memT, MEM, epi_k)
                for c0 in range(0, D, 512):
                    def epi_v(j, bk, bb):
                        evac_copy(vx_sb[:], bk, [bb], [B_vx])
                        sch.dma("sp", VX[j * 128:(j + 1) * 128, c0:c0 + 512], vx_sb[:], reads=[B_vx], writes=[B_VX])
                    gemm_tm(w_xv, D, c0, 512, memT, B_memT, MEM // 128, epi_v)

            sch.barrier()
            zfT = sb1("zfT", [128, FW // 128, TT], BF16)
            B_zfT = Buf("zfT")
            pq_sb = [sb1(f"pq_sb{i}", [128, G, 512], BF16) for i in range(2)]
            B_pq = [Buf(f"pq_sb{i}") for i in range(2)]
            ckn = sb1("ckn", [128, KVR], BF16)
            B_ckn = Buf("ckn")
            ckT_sb = sb1("ckT_sb", [128, KVC, TT], BF16)
            B_ckT = Buf("ckT_sb")
            krr = sb1("krr", [128, 64], BF16)
            B_krr = Buf("krr")
            krT_sb = sb1("krT_sb", [64, 1, TT], BF16)
            B_krT = Buf("krT_sb")
            rt = sb1("rt", [128, 4, 32], F32)
            B_rt = Buf("rt")
            pq_i = [0]

            def rope_tables(gchunk, par, j):
                pc = posf[:, gchunk:gchunk + 1]
                C1 = 6.28125
                C2 = float(np.float32(TWO_PI - 6.28125))
                MAGIC = 12582912.0
                a = tr[:, 0, :]
                k = tr[:, 1, :]
                r = tr[:, 2, :]
                rc = tr[:, 3, :]
                m = tr[:, 4, :]
                R = [B_tr, B_const]
                W = [B_tr]
                sch.op("dve", "tensor_scalar", out=a, in0=invf[:], scalar1=pc, scalar2=None, op0=ALU.mult, reads=R, writes=W)
                sch.op("dve", "tensor_scalar", out=k, in0=a, scalar1=1.0 / TWO_PI, scalar2=MAGIC, op0=ALU.mult, op1=ALU.add, reads=R, writes=W)
                sch.op("dve", "tensor_scalar", out=k, in0=k, scalar1=MAGIC, scalar2=None, op0=ALU.subtract, reads=R, writes=W)
                sch.op("dve", "scalar_tensor_tensor", out=r, in0=k, scalar=-C1, in1=a, op0=ALU.mult, op1=ALU.add, reads=R, writes=W)
                sch.op("dve", "scalar_tensor_tensor", out=r, in0=k, scalar=-C2, in1=r, op0=ALU.mult, op1=ALU.add, reads=R, writes=W)
                sch.op("dve", "tensor_scalar", out=rc, in0=r, scalar1=math.pi / 2, scalar2=None, op0=ALU.add, reads=R, writes=W)
                sch.op("dve", "tensor_scalar", out=m, in0=rc, scalar1=math.pi, scalar2=-TWO_PI, op0=ALU.is_gt, op1=ALU.mult, reads=R, writes=W)
                sch.op("dve", "tensor_tensor", out=rc, in0=rc, in1=m, op=ALU.add, reads=R, writes=W)
                LIM = 3.1415925
                for t_ in (r, rc):
                    sch.op("dve", "tensor_scalar", out=t_, in0=t_, scalar1=LIM, scalar2=-LIM, op0=ALU.min, op1=ALU.max, reads=R, writes=W)
                sch.op("act", "activation", out=cs4[:, par, j, 0, :], in_=rc, func=AF.Sin, reads=[B_tr], writes=[B_cs4])
                sch.op("act", "activation", out=cs4[:, par, j, 1, :], in_=r, func=AF.Sin, reads=[B_tr], writes=[B_cs4])

            actT2 = sb1("actT2", [128, DC, TT], BF16)
            uTs = [(actT, B_act), (actT2, Buf("actT2"))]

            def chunk_pre(t0, j, par):
                r0 = t0 + j * 128
                loaded = load_rows(lambda half: xb[r0:r0 + 128, half * HC:(half + 1) * HC])
                outs = norm_pre(lambda half: loaded[half], False)
                rope_tables(r0 // 128, par, j)
                return outs

            def chunk_post(outs, j, ub):
                norm_post(outs, gmix, ub[0], ub[1], slice(j * 128, (j + 1) * 128))

            def load_uT(t0, with_rope):
                for j in range(4):
                    chunk_post(chunk_pre(t0, j, 0), j, uTs[0])

            for j in range(4):
                chunk_post(chunk_pre(0, j, 0), j, uTs[0])
            for tt in range(NT_B):
                t0 = tt * TT
                actT_c, B_act_c = uTs[tt % 2]
                nxt = tt + 1
                pend = {}
                nhooks = FW // 512 + 3
                hook_i = [0]

                def hook():
                    hi = hook_i[0]
                    hook_i[0] += 1
                    if nxt >= NT_B:
                        return
                    evs = [e for e in range(5) if min(e, nhooks - 1) == hi]
                    for e in evs:
                        if e >= 1:
                            chunk_post(pend.pop(e - 1), e - 1, uTs[nxt % 2])
                        if e <= 3:
                            pend[e] = chunk_pre(nxt * TT, e, nxt % 2)
                for c0 in range(0, FW, 512):
                    def epi_z(m, bk, bb):
                        evac_copy(zfT[:, c0 // 128 + m, :], bk, [bb], [B_zfT])
                    gemm_fm(w_in, D, c0, 512, actT_c, B_act_c, TT, epi_z)
                    hook()
                for j in range(4):
                    i = pq_i[0] % 2
                    pq_i[0] += 1
                    for g in range(G):
                        bk, bb = bank()
                        for cc in range(2):
                            sch.op("pe", "matmul", bk, lhsT=zfT[:, 2 * g + cc, j * 128:(j + 1) * 128], rhs=AB[:, g, cc, :],
                                   start=(cc == 0), stop=(cc == 1), reads=[B_zfT, B_AB], writes=[bb])
                        evac_copy(pq_sb[i][:, g, :], bk, [bb], [B_pq[i]])
                    r0 = t0 + j * 128
                    sch.dma("sp", PQ[r0:r0 + 128, :], pq_sb[i][:].rearrange("p g c -> p (g c)"), reads=[B_pq[i]], writes=[B_PQ])
                hook()

                def epi_kv(j, bk, bb):
                    sch.op("act", "activation", out=ckn[:], in_=bk[:, 0:KVR], func=AF.Square, accum_out=stat[:, 8:9],
                           reads=[bb], writes=[B_ckn, B_stat])
                    rstd_from_ss(stat[:, 8:9], KVR, stat[:, 9:10])
                    sch.op("dve", "tensor_scalar", out=ckn[:], in0=bk[:, 0:KVR], scalar1=stat[:, 9:10], scalar2=None, op0=ALU.mult,
                           reads=[bb, B_stat], writes=[B_ckn])
                    transpose_blocks(lambda k: ckn[:, k * 128:(k + 1) * 128], KVC,
                                     lambda i0, n: ckT_sb[:, i0:i0 + n, j * 128:(j + 1) * 128],
                                     lambda i0, n: gkv[:, i0:i0 + n], [B_ckn], B_ckT)
                gemm_tm(w_in, D, FW + QR, KVR, actT_c, B_act_c, 4, epi_kv)
                hook()
                sch.dma("sp", CKT[:, t0:t0 + TT].rearrange("(c p) t -> p c t", p=128), ckT_sb[:], reads=[B_ckT], writes=[B_CKT])

                if tt < NT_O:
                    cqst = pq_sb[1][:].rearrange("p g c -> p (g c)").bitcast(F32)
                    assert G * 256 >= 4 * 512 or QR <= 512
                    for c0 in range(0, QR, 512):
                        w_ = min(512, QR - c0)

                        def epi_cq(j, bk, bb):
                            st_ = cqst[:, (j % (G * 256 // 512)) * 512:(j % (G * 256 // 512)) * 512 + w_]
                            evac_copy(st_, bk[:, 0:w_], [bb], [B_pq[1]])
                            sch.dma("sp", CQ[t0 + j * 128:t0 + (j + 1) * 128, c0:c0 + w_], st_, reads=[B_pq[1]], writes=[B_CQ])
                        gemm_tm(w_in, D, FW + c0, w_, actT_c, B_act_c, 4, epi_cq)

                def epi_kr(j, bk, bb):
                    cos = cs4[:, tt % 2, j, 0, :]
                    sin = cs4[:, tt % 2, j, 1, :]
                    t1 = bk[:, 0:32]
                    t2 = bk[:, 32:64]
                    R = [bb, B_cs4, B_rt]
                    sch.op("dve", "tensor_tensor", out=rt[:, 0, :], in0=t1, in1=cos, op=ALU.mult, reads=R, writes=[B_rt])
                    sch.op("dve", "tensor_tensor", out=rt[:, 1, :], in0=t2, in1=sin, op=ALU.mult, reads=R, writes=[B_rt])
                    sch.op("dve", "tensor_tensor", out=rt[:, 2, :], in0=t1, in1=sin, op=ALU.mult, reads=R, writes=[B_rt])
                    sch.op("dve", "tensor_tensor", out=rt[:, 3, :], in0=t2, in1=cos, op=ALU.mult, reads=R, writes=[B_rt])
                    sch.op("dve", "tensor_tensor", out=krr[:, 0:32], in0=rt[:, 0, :], in1=rt[:, 1, :], op=ALU.subtract, reads=[B_rt], writes=[B_krr])
                    sch.op("dve", "tensor_tensor", out=krr[:, 32:64], in0=rt[:, 2, :], in1=rt[:, 3, :], op=ALU.add, reads=[B_rt], writes=[B_krr])
                    transpose_blocks(lambda k: krr[:, :], 1, lambda i0, n: krT_sb[:, 0:1, j * 128:(j + 1) * 128], None,
                                     [B_krr], B_krT, npart=64)
                gemm_tm(w_in, D, FW + QR + KVR, 64, actT_c, B_act_c, 4, epi_kr)
                hook()
                sch.dma("sp", KRT[:, t0:t0 + TT], krT_sb[:, 0, :], reads=[B_krT], writes=[B_KRT])

        sch.barrier()
        with ExitStack() as p1b:
            def sb1b(name, shape, dt):
                return p1b.enter_context(nc.sbuf_tensor(name, list(shape), dt))

            cq_sb = sb1b("cq_sb", [128, 4, QR], F32)
            B_cq = Buf("cq_sb")
            cqn = sb1b("cqn", [128, QR], BF16)
            B_cqn = Buf("cqn")
            cqT = sb1b("cqT", [128, QC, TT], BF16)
            B_cqT = Buf("cqT")
            q_bf = sb1b("q_bf", [128, 4, H, 192], BF16)
            B_qsb = Buf("q_bf")
            qb_r = sb1b("qb_r", [128, H, 64], BF16)
            B_qb = Buf("qb")
            qt4 = sb1b("qt4", [128, H, 4, 32], F32)
            B_qt4 = Buf("qt4")
            qnT_sb = sb1b("qnT_sb", [128, H, 128], BF16)
            B_qnT = Buf("qnT_sb")
            qrT_sb = sb1b("qrT_sb", [64, H, 128], BF16)
            B_qrT = Buf("qrT_sb")
            q_flat = q_bf[:].rearrange("p j h d -> p j (h d)")
            for tt in range(NT_O):
                t0 = tt * TT
                for j in range(4):
                    r0 = t0 + j * 128
                    sch.dma("sp", cq_sb[:, j, :], CQ[r0:r0 + 128, :], reads=[B_CQ], writes=[B_cq])
                    rope_tables(r0 // 128, 0, j)
                for j in range(4):
                    sch.op("act", "activation", out=cqn[:], in_=cq_sb[:, j, :], func=AF.Square, accum_out=stat[:, 10:11],
                           reads=[B_cq], writes=[B_cqn, B_stat])
                    rstd_from_ss(stat[:, 10:11], QR, stat[:, 12:13])
                    sch.op("dve", "tensor_scalar", out=cqn[:], in0=cq_sb[:, j, :], scalar1=stat[:, 12:13], scalar2=None, op0=ALU.mult,
                           reads=[B_cq, B_stat], writes=[B_cqn])
                    transpose_blocks(lambda k: cqn[:, k * 128:(k + 1) * 128], QC,
                                     lambda i0, n: cqT[:, i0:i0 + n, j * 128:(j + 1) * 128],
                                     lambda i0, n: gq[:, i0:i0 + n], [B_cqn], B_cqT)
                for c0 in range(0, QW, 512):
                    w_ = min(512, QW - c0)

                    def epi_uq(j, bk, bb):
                        evac_copy(q_flat[:, j, c0:c0 + w_], bk[:, 0:w_], [bb], [B_qsb])
                    gemm_tm(w_uq, QR, c0, w_, cqT, B_cqT, 4, epi_uq)
                for j in range(4):
                    cosb = cs4[:, 0, j, 0, :].unsqueeze(1).to_broadcast([128, H, 32])
                    sinb = cs4[:, 0, j, 1, :].unsqueeze(1).to_broadcast([128, H, 32])
                    t1 = q_bf[:, j, :, 128:160]
                    t2 = q_bf[:, j, :, 160:192]
                    R = [B_qsb, B_cs4, B_qt4]
                    sch.op("dve", "tensor_tensor", out=qt4[:, :, 0, :], in0=t1, in1=cosb, op=ALU.mult, reads=R, writes=[B_qt4])
                    sch.op("dve", "tensor_tensor", out=qt4[:, :, 1, :], in0=t2, in1=sinb, op=ALU.mult, reads=R, writes=[B_qt4])
                    sch.op("dve", "tensor_tensor", out=qt4[:, :, 2, :], in0=t1, in1=sinb, op=ALU.mult, reads=R, writes=[B_qt4])
                    sch.op("dve", "tensor_tensor", out=qt4[:, :, 3, :], in0=t2, in1=cosb, op=ALU.mult, reads=R, writes=[B_qt4])
                    sch.op("dve", "tensor_tensor", out=qb_r[:, :, 0:32], in0=qt4[:, :, 0, :], in1=qt4[:, :, 1, :], op=ALU.subtract,
                           reads=[B_qt4], writes=[B_qb])
                    sch.op("dve", "tensor_tensor", out=qb_r[:, :, 32:64], in0=qt4[:, :, 2, :], in1=qt4[:, :, 3, :], op=ALU.add,
                           reads=[B_qt4], writes=[B_qb])
                    transpose_blocks(lambda h: q_bf[:, j, h, 0:128], H, lambda i0, n: qnT_sb[:, i0:i0 + n, :], None, [B_qsb], B_qnT)
                    transpose_blocks(lambda h: qb_r[:, h, :], H, lambda i0, n: qrT_sb[:, i0:i0 + n, :], None, [B_qb], B_qrT, npart=64)
                    c_lo = t0 + j * 128
                    sch.dma("sp", QNT[:, :, c_lo:c_lo + 128].rearrange("h p t -> p h t"), qnT_sb[:], reads=[B_qnT], writes=[B_QNT])
                    sch.dma("sp", QRT[:, :, c_lo:c_lo + 128].rearrange("h p t -> p h t"), qrT_sb[:], reads=[B_qrT], writes=[B_QRT])

        sch.barrier()
        with ExitStack() as p2:
            def sb2(name, shape, dt):
                return p2.enter_context(nc.sbuf_tensor(name, list(shape), dt))

            N1 = S // 128
            K2 = 2 * N1
            assert K2 <= 128
            X2 = sb2("X2", [128, 128, 128], BF16)
            Xb = [actT[:].rearrange("p c t -> p (c t)")[:, 0:128 * 128].rearrange("p (a b) -> p a b", b=128) if DC * TT >= 128 * 128 else None, X2[:]]
            if Xb[0] is None:
                X3 = sb2("X3", [128, 128, 128], BF16)
                Xb[0] = X3[:]
            B_X = [Buf("X0"), Buf("X1")]
            A_re = slab_t[0][:].rearrange("p c n -> p (c n)")[:, 0:N1 * 128].rearrange("p (k c) -> p k c", c=128)
            A_im = slab_t[1][:].rearrange("p c n -> p (c n)")[:, 0:N1 * 128].rearrange("p (k c) -> p k c", c=128)
            B_A = Buf("A")
            fa = sb2("fa", [128, 128], BF16)
            gt = sb2("gt", [128, N1, 2, 32], BF16)
            B_tab = Buf("ffttab")
            sch.dma("sp", fa[0:K2, 0:K2], c_fa, writes=[B_tab])
            sch.dma("sp", gt[:], c_gt, writes=[B_tab])
            yT_sb = sb2("yT_sb", [128, OWN], BF16)
            B_yT = Buf("yT_sb")
            yst = sb2("yst", [128, OWN // 128, 128], F32)
            B_yst = Buf("yst")
            ncb = 512 // K2
            ngrp = FW // 128
            for cg in range(ngrp):
                g, h2 = cg // 2, cg % 2
                cP = g * 512 + h2 * 128
                cQ = cP + 256
                xb_ = Xb[cg % 2]
                bx = B_X[cg % 2]
                sch.dma("sp", xb_[0:N1], PQ[:, cP:cP + 128].rearrange("(a b) c -> a b c", b=128), reads=[B_PQ], writes=[bx])
                sch.dma("sp", xb_[N1:K2], PQ[:, cQ:cQ + 128].rearrange("(a b) c -> a b c", b=128), reads=[B_PQ], writes=[bx])
                for c0 in range(0, 128, ncb):
                    bk, bb = bank()
                    for ci in range(ncb):
                        sch.op("pe", "matmul", bk[:, ci * K2:(ci + 1) * K2], lhsT=xb_[0:K2, :, c0 + ci], rhs=fa[0:K2, 0:K2],
                               start=True, stop=True, reads=[bx, B_tab], writes=[bb])
                    v = bk[:, 0:ncb * K2].rearrange("p (c r k) -> p c r k", r=2, k=N1)
                    sch.op("act", "activation", out=A_re[:, :, c0:c0 + ncb], in_=v[:, :, 0, :].rearrange("p c k -> p k c"), func=AF.Copy,
                           reads=[bb], writes=[B_A])
                    sch.op("dve", "tensor_copy", out=A_im[:, :, c0:c0 + ncb], in_=v[:, :, 1, :].rearrange("p c k -> p k c"),
                           reads=[bb], writes=[B_A])
                yv = yT_sb[:].rearrange("p (b a) -> p a b", a=N1)
                for k0 in range(0, N1, 16):
                    bk, bb = bank()
                    for kk in range(16):
                        k1 = k0 + kk
                        sch.op("pe", "matmul", bk[:, kk * 32:(kk + 1) * 32], lhsT=A_re[:, k1, :], rhs=gt[:, k1, 0, :],
                               start=True, stop=False, reads=[B_A, B_tab], writes=[bb])
                        sch.op("pe", "matmul", bk[:, kk * 32:(kk + 1) * 32], lhsT=A_im[:, k1, :], rhs=gt[:, k1, 1, :],
                               start=False, stop=True, reads=[B_A, B_tab], writes=[bb])
                    evac_copy(yv[:, k0:k0 + 16, :], bk.rearrange("p (a b) -> p a b", b=32), [bb], [B_yT])
                nblk = OWN // 128
                i0 = 0
                while i0 < nblk:
                    n = min(8, nblk - i0)
                    bk, bb = bank()
                    bkb = bk.bitcast(BF16)
                    for j in range(n):
                        sch.op("pe", "transpose", out=bkb[:, j * 128:(j + 1) * 128], in_=yT_sb[:, (i0 + j) * 128:(i0 + j + 1) * 128],
                               identity=ident_b[:], reads=[B_yT, B_const], writes=[bb])
                    evac_copy(yst[:, i0:i0 + n, :], bkb[:, 0:n * 128].rearrange("p (n c) -> p n c", c=128), [bb], [B_yst])
                    i0 += n
                sch.dma("sp", Y[:, cg * 128:(cg + 1) * 128].rearrange("(j p) c -> p j c", p=128), yst[:], reads=[B_yst], writes=[B_Y])

        sch.barrier()
        with ExitStack() as p3:
            def sb3(name, shape, dt):
                return p3.enter_context(nc.sbuf_tensor(name, list(shape), dt))

            ckT = sb3("ckT", [128, KVC, S], BF16)
            B_ckTr = Buf("ckT")
            krT = xs_all[:].rearrange("p a n -> p (a n)")[:, 0:S]
            assert 4 * HC >= S
            B_krTr = Buf("krT")
            sch.dma("sp", ckT[:], CKT.rearrange("(c p) t -> p c t", p=128), reads=[B_CKT], writes=[B_ckTr])
            sch.op("dve", "memset", krT[64:128, :], 0.0, writes=[B_krTr])
            sch.dma("sp", krT[0:64, :], KRT, reads=[B_KRT], writes=[B_krTr])
            casts = []
            B_wb = {}
            for nm, src, dst in (("out", w_out, w_out_b), ("xq", w_xq, w_xq_b), ("xo", w_xo, w_xo_b),
                                 ("ff1", w_ff1, w_ff1_b), ("ff2", w_ff2, w_ff2_b)):
                B_wb[nm] = Buf("wb_" + nm)
                R_, C_ = src.shape
                n_ = min(2048, C_)
                rr = max(1, (4 * 1024 * 1024) // C_)
                for r0 in range(0, R_, rr):
                    r1 = min(R_, r0 + rr)
                    casts.append((dst[r0:r1, :].rearrange("r (a n) -> (r a) n", n=n_),
                                  src[r0:r1, :].rearrange("r (a n) -> (r a) n", n=n_), B_wb[nm]))
            ncast_per = (len(casts) + H - 1) // H
            wkv = [sb3(f"wkv{i}", [128, KVC, 256], BF16) for i in range(2)]
            B_wkv = [Buf(f"wkv{i}") for i in range(2)]
            KT = [slab_t[i][:].rearrange("p c n -> p (c n)")[:, 0:S] for i in range(2)]
            B_KT = [Buf(f"KT{i}") for i in range(2)]
            Vh = [actT[:, i * (DC // 2):(i + 1) * (DC // 2), :].rearrange("p c t -> p (c t)")[:, 0:S].rearrange("p (c d) -> p c d", d=128)
                  for i in range(2)]
            B_Vh = [Buf(f"Vh{i}") for i in range(2)]
            qn_t = [sb3(f"qn_t{i}", [128, TT], BF16) for i in range(2)]
            qr_t = [sb3(f"qr_t{i}", [128, TT], BF16) for i in range(2)]
            B_q = [Buf(f"q_t{i}") for i in range(2)]
            for i in range(2):
                sch.op("dve", "memset", qr_t[i][64:128, :], 0.0, writes=[B_q[i]])
            NP = 3
            pT = [sb3(f"pT{i}", [128, TT], BF16) for i in range(NP)]
            B_pT = [Buf(f"pT{i}") for i in range(NP)]
            rec = sb3("rec", [128, TT], F32)
            B_rec = Buf("rec")
            on = sb3("on", [128, TT], F32)
            B_on = Buf("on")
            ot = sb3("ot", [128, 4, 128], F32)
            B_ot = Buf("ot")
            qi = 0
            pi_ = 0
            qki = 0
            nck = S // 128
            bank_set[0] = [6, 7]
            for h in range(H):
                hb = (h % 2) if not cfg.get("hb0") else 0
                sch.dma("pool", wkv[hb][:], w_ukv[:, h * 256:(h + 1) * 256].rearrange("(c p) n -> p c n", p=128), writes=[B_wkv[hb]])
                for (cdst, csrc, cbuf) in casts[h * ncast_per:(h + 1) * ncast_per]:
                    sch.dma("pool", cdst, csrc, writes=[cbuf])
                for tk in range(S // 512):
                    bk, bb = bank()
                    for kc in range(KVC):
                        sch.op("pe", "matmul", bk, lhsT=wkv[hb][:, kc, 0:128], rhs=ckT[:, kc, tk * 512:(tk + 1) * 512],
                               start=(kc == 0), stop=(kc == KVC - 1), reads=[B_wkv[hb], B_ckTr], writes=[bb])
                    evac_copy(KT[hb][:, tk * 512:(tk + 1) * 512], bk, [bb], [B_KT[hb]])
                for c4 in range(nck // 4):
                    bk, bb = bank()
                    for ci in range(4):
                        c = c4 * 4 + ci
                        for kc in range(KVC):
                            sch.op("pe", "matmul", bk[:, ci * 128:(ci + 1) * 128], lhsT=ckT[:, kc, c * 128:(c + 1) * 128],
                                   rhs=wkv[hb][:, kc, 128:256], start=(kc == 0), stop=(kc == KVC - 1),
                                   reads=[B_wkv[hb], B_ckTr], writes=[bb])
                    evac_copy(Vh[hb][:, c4 * 4:(c4 + 1) * 4, :], bk.rearrange("p (c d) -> p c d", d=128), [bb], [B_Vh[hb]])
                for tq in range(NT_O):
                    qb = qi % 2
                    qi += 1
                    sch.dma("sp", qn_t[qb][:], QNT[h, :, tq * TT:(tq + 1) * TT], reads=[B_QNT], writes=[B_q[qb]])
                    sch.dma("sp", qr_t[qb][0:64, :], QRT[h, :, tq * TT:(tq + 1) * TT], reads=[B_QRT], writes=[B_q[qb]])
                    o_bk, o_bb = bank_at(0)
                    s_bk, s_bb = bank_at(1)

                    def qk(c):
                        nonlocal qki
                        bk, bb = bank_at(2 + qki % 4)
                        qki += 1
                        sch.op("pe", "matmul", bk, lhsT=KT[hb][:, c * 128:(c + 1) * 128], rhs=qn_t[qb][:], start=True, stop=False,
                               reads=[B_KT[hb], B_q[qb]], writes=[bb])
                        sch.op("pe", "matmul", bk, lhsT=krT[:, c * 128:(c + 1) * 128], rhs=qr_t[qb][:], start=False, stop=True,
                               reads=[B_krTr, B_q[qb]], writes=[bb])
                        return bk, bb
                    sch.op("pe", "matmul", s_bk[:, 0:4], lhsT=ones_b[:], rhs=zeros_b[:, 0:4], start=True, stop=False, skip_group_check=True,
                           reads=[B_const], writes=[s_bb])
                    pend_qk = [qk(cc) for cc in range(min(3, nck))]
                    for c in range(nck):
                        cur = pend_qk.pop(0)
                        if c + 3 < nck:
                            pend_qk.append(qk(c + 3))
                        pb = pi_ % NP
                        pi_ += 1
                        sch.op("act", "activation", out=pT[pb][:], in_=cur[0], func=AF.Exp, scale=att_scale,
                               reads=[cur[1]], writes=[B_pT[pb]])
                        sch.op("pe", "matmul", o_bk, lhsT=Vh[hb][:, c, :], rhs=pT[pb][:], start=(c == 0), stop=(c == nck - 1),
                               reads=[B_Vh[hb], B_pT[pb]], writes=[o_bb])
                        for sub in range(4):
                            sch.op("pe", "matmul", s_bk[:, sub:sub + 1], lhsT=pT[pb][:, sub * 128:(sub + 1) * 128], rhs=ones_b[:, 0:1],
                                   start=False, stop=(c == nck - 1), skip_group_check=True,
                                   reads=[B_const, B_pT[pb]], writes=[s_bb])
                    sch.op("dve", "reciprocal", out=rec[:, 0:4], in_=s_bk[:, 0:4], reads=[s_bb], writes=[B_rec])
                    evac_copy(on[:], o_bk, [o_bb], [B_on])
                    tb, tbb = bank()
                    for j in range(4):
                        sch.op("pe", "transpose", out=tb[:, j * 128:(j + 1) * 128], in_=on[:, j * 128:(j + 1) * 128], identity=ident_f[:],
                               reads=[B_on, B_const], writes=[tbb])
                    for j in range(4):
                        sch.op("dve", "tensor_scalar", out=ot[:, j, :], in0=tb[:, j * 128:(j + 1) * 128], scalar1=rec[:, j:j + 1], scalar2=None,
                               op0=ALU.mult, reads=[tbb, B_rec], writes=[B_ot])
                    r0 = tq * TT
                    sch.dma("sp", Y[r0:r0 + TT, FW + h * 128:FW + (h + 1) * 128].rearrange("(j p) d -> p j d", p=128), ot[:],
                            reads=[B_ot], writes=[B_Y])
            bank_set[0] = list(range(8))

        sch.barrier()
        with ExitStack() as p4:
            def sb4(name, shape, dt):
                return p4.enter_context(nc.sbuf_tensor(name, list(shape), dt))

            h_sb = sb4("h_sb", [128, 4, D], F32)
            B_h = Buf("h")
            xp = [sb4("xp0", [128, 512], F32)]
            B_xp = [Buf("xp0")]
            qxT = sb4("qxT", [128, XC, TT], BF16)
            B_qx = Buf("qxT")
            oxT = qxT
            B_ox = B_qx
            assert XC * MEM * 2 <= HC * 4 and (MEM // 128) * XHD * 2 <= HC * 4
            kxh = rows[0][:].bitcast(BF16)[:, 0:XC * MEM].rearrange("p (c m) -> p c m", m=MEM)
            B_kxh = B_rows[0]
            vxh = rows[1][:].bitcast(BF16)[:, 0:(MEM // 128) * XHD].rearrange("p (c e) -> p c e", e=XHD)
            B_vxh = B_rows[1]
            pX = [sb4(f"pX{i}", [128, TT], BF16) for i in range(2)]
            B_pX = [Buf(f"pX{i}") for i in range(2)]
            recx = sb4("recx", [128, TT], F32)
            B_recx = Buf("recx")
            aT0 = sb4("aT0", [128, FBC, TT], BF16)
            if XC >= FBC:
                aT = [aT0, qxT]
                B_aT = [Buf("aT0"), B_qx]
            else:
                aT1 = sb4("aT1", [128, FBC, TT], BF16)
                aT = [aT0, aT1]
                B_aT = [Buf("aT0"), Buf("aT1")]
            rl = [recx]
            B_rl = [B_recx]
            xpi = 0
            rli = 0

            def norm_T_from_h(gT):
                for j in range(4):
                    norm_T_rows(lambda half: (h_sb[:, j, half * HC:(half + 1) * HC], B_h), gT, actT, B_act,
                                slice(j * 128, (j + 1) * 128), False, ())

            for tt in range(NT_O):
                t0 = tt * TT
                for j in range(4):
                    r0 = t0 + j * 128
                    norm_T_from_dram(lambda half: Y[r0:r0 + 128, half * HC:(half + 1) * HC],
                                     gy, actT, B_act, slice(j * 128, (j + 1) * 128), True, src_bufs=[B_Y])
                for c0 in range(0, D, 512):
                    def epi_o(j, bk, bb):
                        nonlocal xpi
                        i = 0
                        xpi += 1
                        r0 = t0 + j * 128
                        sch.dma("sp", xp[i][:], xb[r0:r0 + 128, c0:c0 + 512], writes=[B_xp[i]])
                        sch.op("dve", "tensor_tensor", out=h_sb[:, j, c0:c0 + 512], in0=bk, in1=xp[i][:], op=ALU.add,
                               reads=[bb, B_xp[i]], writes=[B_h])
                    gemm_tm(w_out_b, D, c0, 512, actT, B_act, 4, epi_o, wbuf=[B_wb['out']])
                norm_T_from_h(gxat)
                nm = MEM // 128
                assert nm == 2
                for hd in range(XH):
                    e0 = hd * XHD
                    sch.dma("sp", kxh, KXT[e0:e0 + XHD, :].rearrange("(c p) m -> p c m", p=128), reads=[B_KXT], writes=[B_kxh])
                    sch.dma("sp", vxh, VX[:, e0:e0 + XHD].rearrange("(c p) e -> p c e", p=128), reads=[B_VX], writes=[B_vxh])
                    for c0 in range(0, XHD, 512):
                        w_ = min(512, XHD - c0)

                        def epi_xq(m, bk, bb):
                            evac_copy(qxT[:, c0 // 128 + m, :], bk, [bb], [B_qx])
                        gemm_fm(w_xq_b, D, e0 + c0, w_, actT, B_act, TT, epi_xq, wbuf=[B_wb['xq']])
                    s_bk, s_bb = bank()
                    for mc in range(nm):
                        bk, bb = bank()
                        for ec in range(XC):
                            sch.op("pe", "matmul", bk, lhsT=kxh[:, ec, mc * 128:(mc + 1) * 128], rhs=qxT[:, ec, :],
                                   start=(ec == 0), stop=(ec == XC - 1), reads=[B_kxh, B_qx], writes=[bb])
                        sch.op("act", "activation", out=pX[mc][:], in_=bk, func=AF.Exp, scale=x_scale, reads=[bb], writes=[B_pX[mc]])
                        sch.op("pe", "matmul", s_bk, lhsT=ones_b[:], rhs=pX[mc][:], start=(mc == 0), stop=(mc == nm - 1),
                               reads=[B_const, B_pX[mc]], writes=[s_bb])
                    sch.op("dve", "reciprocal", out=recx[:], in_=s_bk, reads=[s_bb], writes=[B_recx])
                    for ec in range(XC):
                        bk, bb = bank()
                        for mc in range(nm):
                            sch.op("pe", "matmul", bk, lhsT=vxh[:, mc, ec * 128:(ec + 1) * 128], rhs=pX[mc][:],
                                   start=(mc == 0), stop=(mc == nm - 1), reads=[B_vxh, B_pX[0], B_pX[1]], writes=[bb])
                        sch.op("dve", "tensor_tensor", out=oxT[:, ec, :], in0=bk, in1=recx[:], op=ALU.mult,
                               reads=[bb, B_recx], writes=[B_ox])
                    for c0 in range(0, D, 512):
                        def epi_xo(j, bk, bb):
                            sch.op("dve", "tensor_tensor", out=h_sb[:, j, c0:c0 + 512], in0=bk, in1=h_sb[:, j, c0:c0 + 512], op=ALU.add,
                                   reads=[bb, B_h], writes=[B_h])
                        gemm_tm(w_xo_b[e0:e0 + XHD, :], XHD, c0, 512, oxT, B_ox, 4, epi_xo, wbuf=[B_wb['xo']])
                norm_T_from_h(gmlp)
                def ffn1(fb):
                    ab = fb % 2
                    for c0 in range(0, FB, 512):
                        def epi_f1(m, bk, bb):
                            sch.op("act", "activation", out=rl[0][:], in_=bk, func=AF.Relu, reads=[bb], writes=[B_rl[0]])
                            sch.op("act", "activation", out=aT[ab][:, c0 // 128 + m, :], in_=rl[0][:], func=AF.Square,
                                   reads=[B_rl[0]], writes=[B_aT[ab]])
                        gemm_fm(w_ff1_b, D, fb * FB + c0, 512, actT, B_act, TT, epi_f1, wbuf=[B_wb['ff1']])

                def ffn2(fb):
                    ab = fb % 2
                    for c0 in range(0, D, 512):
                        def epi_f2(j, bk, bb):
                            sch.op("dve", "tensor_tensor", out=h_sb[:, j, c0:c0 + 512], in0=bk, in1=h_sb[:, j, c0:c0 + 512], op=ALU.add,
                                   reads=[bb, B_h], writes=[B_h])
                        gemm_tm(w_ff2_b[fb * FB:(fb + 1) * FB, :], FB, c0, 512, aT[ab], B_aT[ab], 4, epi_f2, wbuf=[B_wb['ff2']])

                nfb = DFF // FB
                ffn1(0)
                for fb in range(nfb):
                    if fb + 1 < nfb:
                        ffn1(fb + 1)
                    ffn2(fb)
                for j in range(4):
                    for half in range(2):
                        sch.op("act", "activation", out=xs[:], in_=h_sb[:, j, half * HC:(half + 1) * HC], func=AF.Square,
                               accum_out=stat[:, 5 + half:6 + half], reads=[B_h], writes=[B_xs, B_stat])
                    sch.op("dve", "tensor_tensor", out=stat[:, 4:5], in0=stat[:, 5:6], in1=stat[:, 6:7], op=ALU.add, reads=[B_stat], writes=[B_stat])
                    rstd_from_ss(stat[:, 4:5], D, stat[:, 12 + j:13 + j])
                PW = HC // 4
                for c0 in range(0, D, PW):
                    sch.dma("sp", xp[0][:, 0:PW], g_fin[c0:c0 + PW].partition_broadcast(128), writes=[B_xp[0]])
                    i = rows_i[0] % 2
                    rows_i[0] += 1
                    for j in range(4):
                        sch.op("dve", "scalar_tensor_tensor", out=rows[i][:, j * PW:(j + 1) * PW], in0=h_sb[:, j, c0:c0 + PW],
                               scalar=stat[:, 12 + j:13 + j], in1=xp[0][:, 0:PW], op0=ALU.mult, op1=ALU.mult,
                               reads=[B_h, B_stat, B_xp[0]], writes=[B_rows[i]])
                    sch.dma("sp", out[t0:t0 + TT, c0:c0 + PW].rearrange("(j p) n -> p j n", p=128),
                            rows[i][:, 0:4 * PW].rearrange("p (j n) -> p j n", n=PW), reads=[B_rows[i]], is_output=True)
        sch.finish(block)
    return nc


def _consts(cfg, q):
    S = cfg["S"]
    N1 = S // 128
    ident = np.eye(128, dtype=np.float32)
    half = 32
    invf = (np.float32(10000.0) ** (-np.arange(half, dtype=np.float32) / np.float32(half))).astype(np.float32)
    invf = np.broadcast_to(invf[None, :], (128, 32)).copy()
    j = np.arange(256, dtype=np.int64)
    ang = 2.0 * np.pi * ((j[:, None] * j[None, :]) % 256).astype(np.float64) / 256.0
    sc = 1.0 / math.sqrt(256.0 * S)
    cs = np.concatenate([np.cos(ang) * sc, np.sin(ang) * sc], axis=1).astype(np.float32)
    a1 = np.arange(N1, dtype=np.int64)
    angA = 2.0 * np.pi * ((a1[:, None] * a1[None, :]) % N1).astype(np.float64) / N1
    c, s_ = np.cos(angA), np.sin(angA)
    fa = np.block([[c, -s_], [-s_, -c]]).astype(ml_dtypes.bfloat16)
    s2 = np.arange(128, dtype=np.int64)[:, None, None]
    k1 = np.arange(N1, dtype=np.int64)[None, :, None]
    k2 = (32 * q + np.arange(32, dtype=np.int64))[None, None, :]
    angC = 2.0 * np.pi * ((s2 * (k1 + N1 * k2)) % S).astype(np.float64) / S
    tc, ts = np.cos(angC), np.sin(angC)
    m = (q * np.arange(N1)) % 4
    GR = np.where(m[None, :, None] == 0, tc, np.where(m[None, :, None] == 1, -ts, np.where(m[None, :, None] == 2, -tc, ts)))
    GI = np.where(m[None, :, None] == 0, ts, np.where(m[None, :, None] == 1, tc, np.where(m[None, :, None] == 2, -ts, -tc)))
    gt = np.stack([GR, GI], axis=2).astype(ml_dtypes.bfloat16)
    return ident, invf, cs, fa, np.ascontiguousarray(gt)


def _gT(g):
    g = np.asarray(g, dtype=np.float32).reshape(-1)
    return np.ascontiguousarray(g.reshape(-1, 128).T)


def make_in_maps(cfg, inp):
    S, D = cfg["S"], cfg["D"]
    OWN = S // 4
    f = lambda a: np.ascontiguousarray(np.asarray(a, dtype=np.float32))
    shared = {
        "w_in": f(inp["w_in"][0]),
        "w_f": f(inp["w_fourier"][0]).reshape(-1, 256),
        "w_uq": f(inp["w_uq"][0]).reshape(cfg["QR"], -1),
        "w_ukv": f(inp["w_ukv"][0]).reshape(cfg["KVR"], -1),
        "w_out": f(inp["w_out"][0]),
        "w_xq": f(inp["w_xq"][0]).reshape(D, D),
        "w_xk": f(inp["w_xk"][0]).reshape(D, D),
        "w_xv": f(inp["w_xv"][0]).reshape(D, D),
        "w_xo": f(inp["w_xo"][0]).reshape(D, D),
        "w_ff1": f(inp["w_ff1"][0]),
        "w_ff2": f(inp["w_ff2"][0]),
        "gT_mix": _gT(inp["g_mix"][0]),
        "gT_q": _gT(inp["g_q_lora"][0]),
        "gT_kv": _gT(inp["g_kv_lora"][0]),
        "gT_y": _gT(np.concatenate([np.asarray(inp["g_fourier_out"][0]), np.asarray(inp["g_mla_out"][0])])),
        "gT_xat": _gT(inp["g_xattn"][0]),
        "gT_mem": _gT(inp["g_mem"][0]),
        "gT_mlp": _gT(inp["g_mlp"][0]),
        "g_fin": f(inp["g_final"]),
    }
    x = np.asarray(inp["x"], dtype=np.float32)
    mem = np.asarray(inp["mem"], dtype=np.float32)
    pos = np.asarray(inp["positions"]).astype(np.int32)
    maps = []
    for c in range(8):
        b, q = c // 4, c % 4
        ident, invf, cs, fa, gt = _consts(cfg, q)
        xr = np.ascontiguousarray(np.roll(x[b], -q * OWN, axis=0))
        pr = np.roll(pos[b], -q * OWN)
        m = dict(shared)
        m.update({
            "xb": xr,
            "memb": np.ascontiguousarray(mem[b]),
            "posT": np.ascontiguousarray(pr.reshape(-1, 128).T),
            "c_ident": ident, "c_invf": invf, "c_cs": cs, "c_fa": fa, "c_gt": gt,
        })
        maps.append(m)
    return maps


def run(cfg, inp):
    nc = build_nc(cfg)
    maps = make_in_maps(cfg, inp)
    res = run_bass_kernel_spmd(nc, maps, core_ids=list(range(8)))
    S, D = cfg["S"], cfg["D"]
    OWN = S // 4
    outp = np.zeros((2, S, D), dtype=np.float32)
    for c in range(8):
        b, q = c // 4, c % 4
        outp[b, q * OWN:(q + 1) * OWN] = res.results[c]["out"]
    return outp, res


def kernel(**inputs):
    outp, _ = run(FULL_CFG, inputs)
    return outp
```
